# Optimizing a Trainium2 kernel written in Bass

```python
import math
import jax, jax.numpy as jnp
from jax import lax
import numpy as np

D_MODEL = 1024
BATCH = 16
SEQ = 256
DEPTH = 2
DEC_BATCH = 2
DEC_SEQ = 1024
PAST_LEN = 256

GRID_W = 64
HEAD_DIM = 64
Q_BLOCK = 128
WINDOW = 128
ROPE_THETA = 10000.0
EPS = 1e-6
NEG_INF = -1e30
A_HEADS = D_MODEL // (2 * HEAD_DIM)
A_KV_HEADS = 2
B_HEADS = D_MODEL // (2 * HEAD_DIM)
B_KV_HEADS = 2
C_HEADS = D_MODEL // (2 * HEAD_DIM)
C_VDIM = 2 * HEAD_DIM
A_W = A_HEADS * HEAD_DIM
B_W = B_HEADS * HEAD_DIM
A_KVW = A_KV_HEADS * HEAD_DIM
B_KVW = B_KV_HEADS * HEAD_DIM
C_QW = C_HEADS * 2 * HEAD_DIM
C_W = C_HEADS * C_VDIM
EVEN_IN = 2 * A_W + 2 * A_KVW + 2 * B_W + 2 * B_KVW
ODD_IN = 2 * C_QW + 2 * C_W
N_EVEN = (DEPTH + 1) // 2
N_ODD = DEPTH // 2
ALPHA = (2 * DEPTH) ** 0.25
BETA = (8 * DEPTH) ** -0.25

kernel_name = 'hybrid_diffusion_prefix_trunk_step'


def _split(p, sizes):
    idx = [int(v) for v in np.cumsum(sizes)[:-1]]
    return jnp.split(p, idx, axis=-1)


def rms_norm(x, w):
    xf = x.astype(jnp.float32)
    y = xf * lax.rsqrt(jnp.mean(xf * xf, -1, keepdims=True) + EPS)
    return (y * w.astype(jnp.float32)).astype(x.dtype)


def layer_norm(x, g, b):
    xf = x.astype(jnp.float32)
    mu = jnp.mean(xf, -1, keepdims=True)
    xc = xf - mu
    var = jnp.mean(xc * xc, -1, keepdims=True)
    y = xc * lax.rsqrt(var + EPS) * g.astype(jnp.float32) + b.astype(jnp.float32)
    return y.astype(x.dtype)


def adaln(cvec, w_mod, b_mod):
    m = jax.nn.silu(cvec) @ w_mod + b_mod
    if m.ndim == 2:
        m = m[:, None, :]
    return jnp.split(m, 3, axis=-1)


def axial_angles(n_tokens):
    t = jnp.arange(n_tokens)
    row = (t // GRID_W).astype(jnp.float32)
    col = (t % GRID_W).astype(jnp.float32)
    n_freq = HEAD_DIM // 4
    freqs = ROPE_THETA ** (-jnp.arange(n_freq, dtype=jnp.float32) / n_freq)
    return row[:, None] * freqs, col[:, None] * freqs


def _rot(x, ang):
    m = ang.shape[-1]
    c = jnp.cos(ang)[:, None, :]
    s = jnp.sin(ang)[:, None, :]
    x1, x2 = x[..., :m], x[..., m:]
    return jnp.concatenate([x1 * c - x2 * s, x1 * s + x2 * c], -1)


def rope_2d(x, ang_row, ang_col):
    half = HEAD_DIM // 2
    xf = x.astype(jnp.float32)
    out = jnp.concatenate([_rot(xf[..., :half], ang_row), _rot(xf[..., half:], ang_col)], -1)
    return out.astype(x.dtype)


def dense_attention(q, k, v, sink=None):
    b, sq, kvh, g, dq = q.shape
    nb = sq // Q_BLOCK
    scale = dq ** -0.5
    kf = k.astype(jnp.float32)
    vf = v.astype(jnp.float32)
    qb = q.reshape(b, nb, Q_BLOCK, kvh, g, dq).transpose(1, 0, 2, 3, 4, 5)

    def block(qblk):
        s = jnp.einsum('bqkgd,bskd->bkgqs', qblk.astype(jnp.float32), kf) * scale
        if sink is not None:
            sk = jnp.broadcast_to(sink.astype(jnp.float32)[None, :, :, None, None], s.shape[:-1] + (1,))
            p = jax.nn.softmax(jnp.concatenate([s, sk], -1), axis=-1)[..., :-1]
        else:
            p = jax.nn.softmax(s, axis=-1)
        return jnp.einsum('bkgqs,bskd->bqkgd', p, vf)

    o = lax.map(block, qb)
    return o.transpose(1, 0, 2, 3, 4, 5).reshape(b, sq, kvh, g, -1).astype(q.dtype)


def window_attention(q, k, v, k_ctx, v_ctx, sink):
    b, s, kvh, g, d = q.shape
    nb = s // Q_BLOCK
    scale = d ** -0.5
    pad = ((0, 0), (Q_BLOCK, Q_BLOCK), (0, 0), (0, 0))
    kp = jnp.pad(k.astype(jnp.float32), pad).reshape(b, nb + 2, Q_BLOCK, kvh, d)
    vp = jnp.pad(v.astype(jnp.float32), pad).reshape(b, nb + 2, Q_BLOCK, kvh, d)
    kband = jnp.concatenate([kp[:, :-2], kp[:, 1:-1], kp[:, 2:]], axis=2)
    vband = jnp.concatenate([vp[:, :-2], vp[:, 1:-1], vp[:, 2:]], axis=2)
    qb = q.astype(jnp.float32).reshape(b, nb, Q_BLOCK, kvh, g, d)
    s_loc = jnp.einsum('bnqkgd,bnskd->bnkgqs', qb, kband) * scale
    blk = jnp.arange(nb)[:, None, None]
    i = blk * Q_BLOCK + jnp.arange(Q_BLOCK)[None, :, None]
    j = blk * Q_BLOCK - Q_BLOCK + jnp.arange(3 * Q_BLOCK)[None, None, :]
    valid = (jnp.abs(j - i) <= WINDOW) & (j >= 0) & (j < s)
    s_loc = jnp.where(valid[None, :, None, None], s_loc, NEG_INF)
    s_ctx = jnp.einsum('bnqkgd,bskd->bnkgqs', qb, k_ctx.astype(jnp.float32)) * scale
    sk = jnp.broadcast_to(sink.astype(jnp.float32)[None, None, :, :, None, None], s_loc.shape[:-1] + (1,))
    p = jax.nn.softmax(jnp.concatenate([s_loc, s_ctx, sk], -1), axis=-1)
    n_loc = 3 * Q_BLOCK
    o = (jnp.einsum('bnkgqs,bnskd->bnqkgd', p[..., :n_loc], vband)
         + jnp.einsum('bnkgqs,bskd->bnqkgd', p[..., n_loc:-1], v_ctx.astype(jnp.float32)))
    return o.reshape(b, s, kvh, g, d).astype(q.dtype)


def even_mixer(h, w_in, w_out, q_norm, k_norm, sink, angles, kv_ctx):
    b, s, _ = h.shape
    qa, ka, va, ga, qb, kb, vb, gb = _split(h @ w_in, [A_W, A_KVW, A_KVW, A_W, B_W, B_KVW, B_KVW, B_W])
    qa = rms_norm(qa.reshape(b, s, A_HEADS, HEAD_DIM), q_norm)
    ka = rms_norm(ka.reshape(b, s, A_KV_HEADS, HEAD_DIM), k_norm)
    va = va.reshape(b, s, A_KV_HEADS, HEAD_DIM)
    qb = qb.reshape(b, s, B_HEADS, HEAD_DIM)
    kb = kb.reshape(b, s, B_KV_HEADS, HEAD_DIM)
    vb = vb.reshape(b, s, B_KV_HEADS, HEAD_DIM)
    new_kv = (ka, va, kb, vb)
    sink_g = sink.reshape(B_KV_HEADS, B_HEADS // B_KV_HEADS)
    if angles is None:
        oa = dense_attention(qa.reshape(b, s, A_KV_HEADS, -1, HEAD_DIM), ka, va)
        ob = dense_attention(qb.reshape(b, s, B_KV_HEADS, -1, HEAD_DIM), kb, vb, sink_g)
    else:
        ka_c, va_c, kb_c, vb_c = kv_ctx
        qa = rope_2d(qa, *angles)
        ka_r = rope_2d(ka, *angles)
        qb = rope_2d(qb, *angles)
        kb_r = rope_2d(kb, *angles)
        oa = dense_attention(qa.reshape(b, s, A_KV_HEADS, -1, HEAD_DIM),
                             jnp.concatenate([ka_r, ka_c.astype(ka_r.dtype)], 1),
                             jnp.concatenate([va, va_c.astype(va.dtype)], 1))
        ob = window_attention(qb.reshape(b, s, B_KV_HEADS, -1, HEAD_DIM), kb_r, vb, kb_c, vb_c, sink_g)
    oa = oa.reshape(b, s, A_W) * jax.nn.silu(ga)
    ob = ob.reshape(b, s, B_W) * jax.nn.silu(gb)
    return jnp.concatenate([oa, ob], -1) @ w_out, new_kv


def odd_mixer(h, w_in, w_out, lq1, lk1, lq2, lk2, subln, lam_init, angles, kv_ctx):
    b, s, _ = h.shape
    q, k, v, g = _split(h @ w_in, [C_QW, C_QW, C_W, C_W])
    q = q.reshape(b, s, 2 * C_HEADS, HEAD_DIM)
    k = k.reshape(b, s, 2 * C_HEADS, HEAD_DIM)
    v = v.reshape(b, s, C_HEADS, C_VDIM)
    if angles is not None:
        q = rope_2d(q, *angles)
        k = rope_2d(k, *angles)
    k = k.reshape(b, s, C_HEADS, 2 * HEAD_DIM)
    new_kv = (k, v)
    if kv_ctx is None:
        k_all, v_all = k, v
    else:
        k_all = jnp.concatenate([k, kv_ctx[0].astype(k.dtype)], 1)
        v_all = jnp.concatenate([v, kv_ctx[1].astype(v.dtype)], 1)
    q = q.reshape(b, s, C_HEADS, 1, 2 * HEAD_DIM)
    o1 = dense_attention(q[..., :HEAD_DIM], k_all[..., :HEAD_DIM], v_all)
    o2 = dense_attention(q[..., HEAD_DIM:], k_all[..., HEAD_DIM:], v_all)
    lam = (jnp.exp(jnp.sum(lq1.astype(jnp.float32) * lk1.astype(jnp.float32)))
           - jnp.exp(jnp.sum(lq2.astype(jnp.float32) * lk2.astype(jnp.float32))) + lam_init)
    o = (o1.astype(jnp.float32) - lam * o2.astype(jnp.float32))[:, :, :, 0, :]
    o = rms_norm(o, subln) * (1.0 - lam_init)
    o = o.reshape(b, s, C_W).astype(h.dtype) * jax.nn.silu(g)
    return o @ w_out, new_kv


def setup_inputs(seed: int = 0) -> dict:
    key = jax.random.key(seed)
    ks = jax.random.split(key, 32)
    f32 = jnp.float32
    nrm = lambda k, shape, sc=1.0: (jax.random.normal(k, shape, f32) * sc).astype(f32)
    d = D_MODEL
    return {
        'x_prompt': nrm(ks[0], (BATCH, SEQ, d)),
        'x_sample': nrm(ks[1], (DEC_BATCH, DEC_SEQ, d)),
        'cache_a_k': nrm(ks[2], (DEC_BATCH, N_EVEN, PAST_LEN, A_KV_HEADS, HEAD_DIM)),
        'cache_a_v': nrm(ks[3], (DEC_BATCH, N_EVEN, PAST_LEN, A_KV_HEADS, HEAD_DIM)),
        'cache_b_k': nrm(ks[4], (DEC_BATCH, N_EVEN, PAST_LEN, B_KV_HEADS, HEAD_DIM)),
        'cache_b_v': nrm(ks[5], (DEC_BATCH, N_EVEN, PAST_LEN, B_KV_HEADS, HEAD_DIM)),
        'cache_c_k': nrm(ks[6], (DEC_BATCH, N_ODD, PAST_LEN, C_HEADS, 2 * HEAD_DIM)),
        'cache_c_v': nrm(ks[7], (DEC_BATCH, N_ODD, PAST_LEN, C_HEADS, C_VDIM)),
        'c': nrm(ks[8], (DEC_BATCH, d)),
        'c_ctx': nrm(ks[9], (d,)),
        'w_mod': nrm(ks[10], (DEPTH, d, 3 * d), 0.5 * d ** -0.5),
        'b_mod': nrm(ks[11], (DEPTH, 3 * d), 0.01),
        'ln_g': 1.0 + nrm(ks[12], (DEPTH, d), 0.05),
        'ln_b': nrm(ks[13], (DEPTH, d), 0.01),
        'w_in_even': nrm(ks[14], (N_EVEN, d, EVEN_IN), d ** -0.5),
        'w_out_even': nrm(ks[15], (N_EVEN, A_W + B_W, d), BETA * (A_W + B_W) ** -0.5),
        'q_norm_a': 1.0 + nrm(ks[16], (N_EVEN, HEAD_DIM), 0.05),
        'k_norm_a': 1.0 + nrm(ks[17], (N_EVEN, HEAD_DIM), 0.05),
        'sink_b': nrm(ks[18], (N_EVEN, B_HEADS)),
        'w_in_odd': nrm(ks[19], (N_ODD, d, ODD_IN), d ** -0.5),
        'w_out_odd': nrm(ks[20], (N_ODD, C_W, d), BETA * C_W ** -0.5),
        'lambda_q1': nrm(ks[21], (N_ODD, HEAD_DIM), 0.1),
        'lambda_k1': nrm(ks[22], (N_ODD, HEAD_DIM), 0.1),
        'lambda_q2': nrm(ks[23], (N_ODD, HEAD_DIM), 0.1),
        'lambda_k2': nrm(ks[24], (N_ODD, HEAD_DIM), 0.1),
        'subln_c': 1.0 + nrm(ks[25], (N_ODD, C_VDIM), 0.05),
    }


def reference(x_prompt, x_sample, cache_a_k, cache_a_v, cache_b_k, cache_b_v, cache_c_k, cache_c_v,
              c, c_ctx, w_mod, b_mod, ln_g, ln_b, w_in_even, w_out_even, q_norm_a, k_norm_a, sink_b,
              w_in_odd, w_out_odd, lambda_q1, lambda_k1, lambda_q2, lambda_k2, subln_c):
    x = x_prompt
    a_k, a_v, b_k, b_v, c_k, c_v = [], [], [], [], [], []
    for l in range(DEPTH):
        shift, scale, gate = adaln(c_ctx, w_mod[l], b_mod[l])
        h = x * (1 + scale) + shift
        if l % 2 == 0:
            e = l // 2
            out, (ka, va, kb, vb) = even_mixer(h, w_in_even[e], w_out_even[e], q_norm_a[e], k_norm_a[e],
                                               sink_b[e], None, None)
            a_k.append(ka); a_v.append(va); b_k.append(kb); b_v.append(vb)
        else:
            o = l // 2
            lam_init = 0.8 - 0.6 * math.exp(-0.3 * l)
            out, (kc, vc) = odd_mixer(h, w_in_odd[o], w_out_odd[o], lambda_q1[o], lambda_k1[o],
                                      lambda_q2[o], lambda_k2[o], subln_c[o], lam_init, None, None)
            c_k.append(kc); c_v.append(vc)
        x = layer_norm(ALPHA * x + gate * out, ln_g[l], ln_b[l])
    y_prompt = x
    new_a_k = jnp.stack(a_k, axis=1)
    new_a_v = jnp.stack(a_v, axis=1)
    new_b_k = jnp.stack(b_k, axis=1)
    new_b_v = jnp.stack(b_v, axis=1)
    new_c_k = jnp.stack(c_k, axis=1)
    new_c_v = jnp.stack(c_v, axis=1)

    angles = axial_angles(x_sample.shape[1])
    x = x_sample
    for l in range(DEPTH):
        shift, scale, gate = adaln(c, w_mod[l], b_mod[l])
        h = x * (1 + scale) + shift
        if l % 2 == 0:
            e = l // 2
            kv_ctx = (cache_a_k[:, e], cache_a_v[:, e], cache_b_k[:, e], cache_b_v[:, e])
            out, _ = even_mixer(h, w_in_even[e], w_out_even[e], q_norm_a[e], k_norm_a[e],
                                sink_b[e], angles, kv_ctx)
        else:
            o = l // 2
            lam_init = 0.8 - 0.6 * math.exp(-0.3 * l)
            out, _ = odd_mixer(h, w_in_odd[o], w_out_odd[o], lambda_q1[o], lambda_k1[o],
                               lambda_q2[o], lambda_k2[o], subln_c[o], lam_init, angles,
                               (cache_c_k[:, o], cache_c_v[:, o]))
        x = layer_norm(ALPHA * x + gate * out, ln_g[l], ln_b[l])
    y_sample = x
    return (y_prompt, y_sample, new_a_k, new_a_v, new_b_k, new_b_v, new_c_k, new_c_v)
```

```python
import math
import numpy as np
import concourse.bass as bass
import concourse.mybir as mybir
from concourse.bass_utils import run_bass_kernel_spmd

F32 = mybir.dt.float32
BF16 = mybir.dt.bfloat16
AF = mybir.ActivationFunctionType
ALU = mybir.AluOpType
AX = mybir.AxisListType

D = 1024
EPS = 1e-6
ALPHA = (2 * 2) ** 0.25
LAM_INIT = 0.8 - 0.6 * math.exp(-0.3 * 1)
BLK = 256
ENGS = ["pe", "act", "dve", "pool", "sp"]


class Reg:
    def __init__(s, ap, toks):
        s.ap, s.toks = ap, toks

    def m(s, f):
        return Reg(f(s.ap), s.toks)


class T:
    def __init__(s, nc, name, shape, dt, arena=None, off=0, tok=None):
        s.shape = list(shape)
        s.dt = dt
        s.isz = 4 if dt == F32 else 2
        n = int(np.prod(shape))
        if arena is None:
            s.h = nc.alloc_sbuf_tensor("sb_" + name, [128, n], dt)
            base = s.h[:, :]
            s.tok = name
            s.boff = 0
        else:
            a = arena[:, off // 2:(off + n * s.isz) // 2]
            base = a.bitcast(F32) if dt == F32 else a
            s.tok = tok
            s.boff = off
        if len(shape) == 1:
            s.full = base
        elif len(shape) == 2:
            s.full = base.rearrange("p (a b) -> p a b", b=shape[1])
        else:
            s.full = base.rearrange("p (a b c) -> p a b c", b=shape[1], c=shape[2])
        st = [1]
        for d in reversed(shape[1:]):
            st.insert(0, st[0] * d)
        s.strides = st

    def __getitem__(s, idx):
        if not isinstance(idx, tuple):
            idx = (idx,)
        ap = s.full[idx]
        lo = hi = 0
        for d, (n, st) in enumerate(zip(s.shape, s.strides)):
            i = idx[d + 1] if d + 1 < len(idx) else slice(None)
            if isinstance(i, int):
                a, b = i, i + 1
            else:
                a = i.start or 0
                b = n if i.stop is None else i.stop
            lo += a * st
            hi += (b - 1) * st
        lo_b = s.boff + lo * s.isz
        hi_b = s.boff + (hi + 1) * s.isz
        return Reg(ap, [(s.tok, k) for k in range(lo_b // BLK, (hi_b - 1) // BLK + 1)])


class Op:
    __slots__ = ("eng", "fn", "kind", "key", "deps", "dwaits", "need_inc", "sig", "ccid")

    def __init__(s, eng, fn, kind, key):
        s.eng, s.fn, s.kind, s.key = eng, fn, kind, key
        s.deps = []
        s.dwaits = {}
        s.need_inc = False
        s.sig = 0
        s.ccid = None


class Sched:
    def __init__(s):
        s.q = {e: [] for e in ENGS}
        s.W = {}
        s.R = {}
        s.dcnt = {}
        s.ncc = 0
        s.out_ops = []

    @staticmethod
    def _toks(lst):
        out = []
        for r in lst:
            if r is None:
                continue
            if isinstance(r, Reg):
                out.extend(r.toks)
            elif isinstance(r, (tuple, str)):
                out.append(r)
            else:
                raise TypeError(type(r))
        return out

    def _dep(s, o, w):
        if w is o:
            return
        if w.kind == "d":
            if o.kind == "d" and o.key == w.key:
                return
            o.dwaits[w.key] = 16 * s.dcnt[w.key]
            t = ("sem", w.key)
            s.R.setdefault(t, []).append(o)
        else:
            if w.eng == "pe" and o.eng == "pe" and o.kind == "c":
                return
            o.deps.append(w)
            w.need_inc = True

    def op(s, eng, fn, reads=(), writes=(), kind="c", key=None):
        o = Op(eng, fn, kind, key)
        rt = s._toks(reads)
        wt = s._toks(writes)
        if eng != "pe":
            wt = wt + [t for t in rt if isinstance(t, tuple) and t[0] == "ps" and t not in wt]
        seen = set()
        for t in rt:
            for w in s.W.get(t, ()):
                if id(w) not in seen:
                    seen.add(id(w))
                    s._dep(o, w)
        for t in wt:
            for w in s.W.get(t, ()):
                if id(w) not in seen:
                    seen.add(id(w))
                    s._dep(o, w)
            for w in s.R.get(t, ()):
                if id(w) not in seen:
                    seen.add(id(w))
                    s._dep(o, w)
        if kind == "d":
            t = ("sem", key)
            for w in s.R.get(t, ()):
                if id(w) not in seen:
                    seen.add(id(w))
                    s._dep(o, w)
            s.dcnt[key] = s.dcnt.get(key, 0) + 1
            s.R[t] = []
        if kind == "cc":
            o.ccid = s.ncc
            s.ncc += 1
        for t in rt:
            s.R.setdefault(t, []).append(o)
        for t in wt:
            s.W[t] = [o]
            s.R[t] = []
        s.q[eng].append(o)
        return o

    def finalize(s):
        for e in ENGS:
            c = 0
            for o in s.q[e]:
                if o.kind == "c" and o.need_inc:
                    c += 1
                    o.sig = c


class _Stop(Exception):
    pass


def build_nc(stop=None):
    nc = bass.Bass("TRN2", target_bir_lowering=False)
    S = Sched()

    def stage(name):
        if stop == name:
            raise _Stop()

    def din(name, shape, dt=F32):
        return nc.dram_tensor(name, shape, dt, kind="ExternalInput").ap()

    def dout(name, shape):
        return nc.dram_tensor(name, shape, F32, kind="ExternalOutput").ap()

    xs = din("xs", [768, D])
    cak, cav, cbk, cbv = (din(n, [256, 128]) for n in ("cak", "cav", "cbk", "cbv"))
    cck, ccv = din("cck", [256, D]), din("ccv", [256, D])
    cvec = din("cvec", [2, D])
    wmod_sh = din("wmod_sh", [2, D, 768])
    bmod_sh = din("bmod_sh", [2, 768])
    ln_g_d, ln_b_d = din("ln_g", [2, D]), din("ln_b", [2, D])
    w_in_e, w_out_e = din("w_in_e", [D, 2560]), din("w_out_e", [D, D])
    w_in_o, w_out_o = din("w_in_o", [D, 4096]), din("w_out_o", [D, D])
    qn_d, kn_d = din("qn", [64]), din("kn", [64])
    sink_d = din("sink", [8])
    lq1_d, lk1_d, lq2_d, lk2_d = (din(n, [64]) for n in ("lq1", "lk1", "lq2", "lk2"))
    subln_d = din("subln", [128])
    ident_d = din("ident", [128, 128])
    rmat_d, bd_d = din("rmat", [128, 128]), din("bdmat", [128, 128])
    cos_d, sin_d = din("cosT", [128, 256]), din("sinT", [128, 256])
    mask_d = din("maskd", [128, 2048])

    y_o = dout("y", [768, D])
    ak_o, av_o, bk_o, bv_o = (dout(n, [512, 128]) for n in ("ak", "av", "bk", "bv"))
    ck_o, cv_o = dout("ck", [512, D]), dout("cv", [512, D])
    import os
    DBG = bool(os.environ.get("KDBG"))
    dbg_o = dout("dbg", [768, D]) if DBG else None

    mg_in = nc.dram_tensor("mg_in", [24, 128], F32)
    mg_out = nc.dram_tensor("mg_out", [96, 128], F32)
    g_in = [nc.dram_tensor("g0_in", [768, 256], BF16), nc.dram_tensor("g1_in", [2048, 256], BF16)]
    g_out = [nc.dram_tensor("g0_out", [4 * 768, 256], BF16), nc.dram_tensor("g1_out", [4 * 2048, 256], BF16)]
    GROWS = [768, 2048]

    x_tm = T(nc, "x_tm", [6, D], F32)
    actT = T(nc, "actT", [8, 768], BF16)
    qT = T(nc, "qT", [8, 768], BF16)
    kT = T(nc, "kT", [8, 768], BF16)
    v_sb = T(nc, "v_sb", [6, 1032], BF16)
    g_sb = T(nc, "g_sb", [6, D], BF16)
    wring = [T(nc, "wr%d" % i, [8, 512], BF16) for i in range(3)]
    ident = T(nc, "ident", [128], F32)
    Rm = T(nc, "Rm", [128], BF16)
    BD = T(nc, "BD", [128], BF16)
    cosT = T(nc, "cosT", [256], F32)
    sinT = T(nc, "sinT", [256], F32)
    maskT = T(nc, "maskT", [2048], BF16)
    modfm = T(nc, "modfm", [96], F32)
    onep = T(nc, "onep", [32], F32)
    small = T(nc, "small", [64], F32)
    esink = T(nc, "esink", [8], F32)
    subln_s = T(nc, "subln_s", [128], F32)
    lamt = T(nc, "lamt", [4, 64], F32)
    kctx = T(nc, "kctx", [8, 256], BF16)
    vctx = T(nc, "vctx", [2, 1032], BF16)
    gate_bc = [T(nc, "gate_bc%d" % v, [D], F32) for v in range(2)]
    lng = T(nc, "lng", [D], F32)
    lnb = T(nc, "lnb", [D], F32)
    ckl = T(nc, "ckl", [2, D], F32)
    stk = T(nc, "stk", [128], F32)
    scT = T(nc, "scT", [16], BF16)
    bmT = T(nc, "bmT", [12], F32)
    resm = T(nc, "resm", [24], F32)
    resT = T(nc, "resT", [128], F32)
    mo = T(nc, "mo", [128], F32)
    wm = [T(nc, "wm0", [8, 768], BF16, arena=qT.h, off=0, tok="qT"),
          T(nc, "wm1", [8, 768], BF16, arena=kT.h, off=0, tok="kT")]
    ARN = 46 * 1024
    arena = nc.alloc_sbuf_tensor("arena", [128, ARN // 2], BF16)

    def AT(name, shape, dt, off):
        t = T(nc, name, shape, dt, arena=arena, off=off, tok="ARENA")
        assert off + int(np.prod(shape)) * t.isz <= ARN, name
        return t

    K = 1024
    kf = [AT("kf%d" % i, [768], F32, i * 3 * K) for i in range(3)]
    ko = [AT("ko%d" % i, [4, 128], F32, 9 * K + i * 2 * K) for i in range(2)]
    vo = [AT("vo%d" % i, [512], F32, 13 * K + i * 2 * K) for i in range(3)]
    sq = [AT("sq%d" % i, [768], BF16, 19 * K + i * 1536) for i in range(3)]
    rs = [AT("rs%d" % i, [768], F32, 24 * K + i * 3 * K) for i in range(3)]
    rt1 = [AT("rt1%d" % i, [256], F32, 33 * K + i * K) for i in range(2)]
    rt2 = [AT("rt2%d" % i, [256], F32, 35 * K + i * K) for i in range(2)]
    kl = [AT("kl%d" % i, [1024], BF16, i * 2 * K) for i in range(3)]
    vl = [AT("vl%d" % i, [8, 129], BF16, 6 * K + i * 2112) for i in range(3)]
    pT = [AT("pT%d" % i, [10, 256], BF16, 13 * K + i * 5 * K) for i in range(3)]
    oa = [AT("oa%d" % i, [2, D], F32, 28 * K + i * 8 * K) for i in range(2)]
    rr = AT("rr", [64], F32, 44 * K)
    o1 = AT("o1", [2, 128], F32, 44 * K + 512)
    ssb = AT("ssb", [64], F32, 45 * K + 512)
    yt = [T(nc, "yt%d" % i, [D], F32) for i in range(2)]
    stts = [T(nc, "stt%d" % i, [64], F32) for i in range(2)]

    ps = nc.alloc_psum_tensor("ps", [128, 8, 512], F32)

    def P(b, lo=0, hi=512, p0=0, p1=128):
        return Reg(ps[p0:p1, b, lo:hi], [("ps", b)])

    rrb = {"n": 0, "s": 0, "pv": 0}

    def bank():
        b = rrb["n"] % 8
        rrb["n"] += 1
        return b

    def bankS():
        b = rrb["s"] % 4
        rrb["s"] += 1
        return b

    def bankA():
        b = rrb.get("a", 0) % 4
        rrb["a"] = rrb.get("a", 0) + 1
        return b

    def bankB():
        b = 4 + rrb.get("b", 0) % 2
        rrb["b"] = rrb.get("b", 0) + 1
        return b

    def bankC():
        b = 6 + rrb.get("c", 0) % 2
        rrb["c"] = rrb.get("c", 0) + 1
        return b

    def bankPV():
        b = 4 + rrb["pv"] % 4
        rrb["pv"] += 1
        return b

    def mm(out, lhsT, rhs, start=True, stop=True):
        S.op("pe", lambda e: e.matmul(out.ap, lhsT.ap, rhs.ap, start=start, stop=stop),
             reads=[lhsT, rhs], writes=[out])

    def tr(out, in_, idr):
        S.op("pe", lambda e: e.transpose(out.ap, in_.ap, idr.ap), reads=[in_, idr], writes=[out])

    def act(out, in_, func, scale=1.0, bias=0.0, eng="act"):
        rd = [in_]
        sc = scale.ap if isinstance(scale, Reg) else scale
        bi = bias.ap if isinstance(bias, Reg) else bias
        if isinstance(scale, Reg):
            rd.append(scale)
        if isinstance(bias, Reg):
            rd.append(bias)
        S.op("act", lambda e: e.activation(out.ap, in_.ap, func, bias=bi, scale=sc), reads=rd, writes=[out])

    def tt(eng, out, in0, in1, op):
        S.op(eng, lambda e: e.tensor_tensor(out.ap, in0.ap, in1.ap, op), reads=[in0, in1], writes=[out])

    def ts(eng, out, in0, s1, s2, op0, op1=None):
        rd = [in0]
        a1 = s1.ap if isinstance(s1, Reg) else s1
        a2 = s2.ap if isinstance(s2, Reg) else s2
        if isinstance(s1, Reg):
            rd.append(s1)
        if isinstance(s2, Reg):
            rd.append(s2)
        if op1 is None:
            S.op(eng, lambda e: e.tensor_scalar(out.ap, in0.ap, a1, None, op0), reads=rd, writes=[out])
        else:
            S.op(eng, lambda e: e.tensor_scalar(out.ap, in0.ap, a1, a2, op0, op1), reads=rd, writes=[out])

    def stt_(out, in0, sc, in1, op0, op1):
        rd = [in0, in1]
        a = sc.ap if isinstance(sc, Reg) else sc
        if isinstance(sc, Reg):
            rd.append(sc)
        S.op("dve", lambda e: e.scalar_tensor_tensor(out.ap, in0.ap, a, in1.ap, op0, op1), reads=rd, writes=[out])

    def cp(eng, out, in_):
        if eng == "act":
            S.op("act", lambda e: e.copy(out.ap, in_.ap), reads=[in_], writes=[out])
        else:
            S.op(eng, lambda e: e.tensor_copy(out.ap, in_.ap), reads=[in_], writes=[out])

    def dma(q, out, in_, key, reads=(), writes=(), is_out=False):
        oa_ = out.ap if isinstance(out, Reg) else out
        ia_ = in_.ap if isinstance(in_, Reg) else in_
        rd = list(reads) + ([in_] if isinstance(in_, Reg) else [])
        wr = list(writes) + ([out] if isinstance(out, Reg) else [])
        o = S.op(q, lambda e: e.dma_start(out=oa_, in_=ia_), reads=rd, writes=wr, kind="d", key=key)
        if is_out:
            S.out_ops.append(o)
        return o

    def recip(out, in_):
        S.op("dve", lambda e: e.reciprocal(out.ap, in_.ap), reads=[in_], writes=[out])

    def memset_(eng, reg, val):
        S.op(eng, lambda e: e.memset(reg.ap, val), writes=[reg])

    def bnstats_(out, in_):
        S.op("dve", lambda e: e.bn_stats(out.ap, in_.ap), reads=[in_], writes=[out])

    def bnaggr_(out, in_):
        S.op("dve", lambda e: e.bn_aggr(out.ap, in_.ap), reads=[in_], writes=[out])

    def treduce_(out, in_):
        S.op("dve", lambda e: e.tensor_reduce(out=out.ap, in_=in_.ap, axis=AX.X, op=ALU.add), reads=[in_], writes=[out])

    def allgather_(groups, din_, dout_, rtok, wtok):
        S.op("pool", lambda e: e.collective_compute("AllGather", ALU.bypass, replica_groups=groups,
                                                    ins=[din_.ap().opt()], outs=[dout_.ap().opt()]),
             reads=[rtok], writes=[wtok], kind="cc")

    memset_("dve", small[:, 5:6], EPS)

    def body():
        dma("sp", ident[:, :], ident_d, "c_id")
        stk_toks = [("stkp", n) for n in range(15)]
        S.op("pool", lambda e: e.memset(stk.full[:, :], 0.0), writes=stk_toks)
        n_ = 0
        for v in range(2):
            dma("sp", Reg(stk.full[8 * v:8 * v + 8, :], [stk_toks[n_]]), cvec[v].rearrange("(k p) -> k p", p=128), "c_stk%d" % n_)
            n_ += 1
        for t in range(6):
            for l in range(2):
                r = 16 + t * 2 + l
                dma("sp", Reg(stk.full[r:r + 1, :], [stk_toks[n_]]), bmod_sh[l:l + 1, t * 128:(t + 1) * 128], "c_stk%d" % n_)
                n_ += 1
        stk_all = Reg(stk.full[0:28, :], stk_toks)
        for l in range(2):
            dma("pool", wm[l][:, :, :], wmod_sh[l].rearrange("(k p) n -> p k n", p=128), "c_wm%d" % l)
        for tt_i in range(6):
            dma("sp", x_tm[:, tt_i, :], xs[tt_i * 128:(tt_i + 1) * 128, :], "x%d" % tt_i)
        for ti, ck_ in enumerate((cak, cbk)):
            cv4 = ckl[:, :, ti * 256:(ti + 1) * 256].m(lambda a: a.rearrange("p t (g d) -> p t g d", d=64))
            for kv in range(2):
                for dup in range(2):
                    g_ = 2 * kv + dup
                    dma("sp", cv4.m(lambda a, g_=g_: a[:, :, g_, :]),
                        ck_[:, kv * 64:(kv + 1) * 64].rearrange("(t p) d -> p t d", p=128), "ckl")
        dma("sp", cosT[:, :], cos_d, "c_cos")
        dma("sp", sinT[:, :], sin_d, "c_sin")
        dma("sp", small[0:64, 0:1], qn_d.rearrange("(p o) -> p o", o=1), "c_sm")
        dma("sp", small[64:128, 0:1], qn_d.rearrange("(p o) -> p o", o=1), "c_sm")
        dma("sp", small[0:64, 1:2], kn_d.rearrange("(p o) -> p o", o=1), "c_sm")
        dma("sp", small[64:128, 1:2], kn_d.rearrange("(p o) -> p o", o=1), "c_sm")
        dma("sp", esink[:, :], sink_d.partition_broadcast(128), "c_sink")
        for i, d_ in enumerate((lq1_d, lk1_d, lq2_d, lk2_d)):
            dma("sp", lamt[:, i, :], d_.partition_broadcast(128), "c_lam")
        dma("sp", subln_s[:, :], subln_d.partition_broadcast(128), "c_subln")

        stage("cm1")
        loads = []

        def wsrc(w, c0, n):
            return w[:, c0:c0 + n].rearrange("(k p) n -> p k n", p=128)

        kslot = []
        for base in (512, 1792):
            for kv in range(2):
                for dup in range(2):
                    kslot.append((len(kslot) * 64, wsrc(w_in_e, base + kv * 64, 64), 64))
        loads.append([(0, wsrc(w_in_e, 0, 512), 512)])
        loads.append(kslot)
        loads.append([(0, wsrc(w_in_e, 1280, 512), 512)])
        loads.append([(0, wsrc(w_in_e, 640, 128), 128), (128, wsrc(w_in_e, 1920, 128), 128)])
        loads.append([(0, wsrc(w_in_e, 768, 512), 512)])
        loads.append([(0, wsrc(w_in_e, 2048, 512), 512)])
        loads.append([(0, wsrc(w_out_e, 0, 512), 512)])
        loads.append([(0, wsrc(w_out_e, 512, 512), 512)])
        for i in range(8):
            loads.append([(0, wsrc(w_in_o, 512 * i, 512), 512)])
        loads.append([(0, wsrc(w_out_o, 0, 512), 512)])
        loads.append([(0, wsrc(w_out_o, 512, 512), 512)])
        wstate = {"next": 0}

        def issue_load():
            i = wstate["next"]
            if i >= len(loads):
                return
            wstate["next"] += 1
            sl = i % 3
            for (c0, src, n) in loads[i]:
                dma("pool", wring[sl][:, :, c0:c0 + n], src, "w%d" % sl)

        def W(i):
            return wring[i % 3]

        stage("c0")
        b0 = bank()
        tr(P(b0, 0, 28), stk_all, ident[0:28, 0:28])
        act(scT[:, :].m(lambda a: a.rearrange("p (k v) -> p v k", v=2)),
            P(b0, 0, 16).m(lambda a: a.rearrange("p (v k) -> p v k", v=2)), AF.Silu)
        cp("dve", bmT[:, :], P(b0, 16, 28))
        stage("a1")
        b1 = bank()
        for l in range(2):
            for t in range(6):
                c0 = t * 4 + l * 2
                for k in range(8):
                    mm(P(b1, c0, c0 + 2), wm[l][:, k, t * 128:(t + 1) * 128], scT[:, 2 * k:2 * k + 2],
                       start=(k == 0), stop=(k == 7))
        tt("dve", resm[:, :].m(lambda a: a.rearrange("p (g v) -> p g v", v=2)),
           P(b1, 0, 24).m(lambda a: a.rearrange("p (g v) -> p g v", v=2)),
           bmT[:, :].m(lambda a: a.unsqueeze(2).to_broadcast([128, 12, 2])), ALU.add)
        stage("a2")
        b2 = bank()
        tr(P(b2, 0, 128, 0, 24), resm[:, :], ident[:, :])
        cp("dve", resT[0:24, :], P(b2, 0, 128, 0, 24))
        dma("sp", mg_in.ap(), resT[0:24, :], "mgi", writes=[("dram", "mg_in")])
        def xT_phase(l, raw):
            for j2 in range(0, 8, 2):
                bs = bank()
                bA = [bank(), bank()]
                for jj in range(2):
                    j = j2 + jj
                    for i, t_ in enumerate((2, 3, 4, 5)):
                        tr(P(bA[jj], i * 128, (i + 1) * 128), x_tm[:, t_, j * 128:(j + 1) * 128], ident[:, :])
                    for t_ in range(2):
                        tr(P(bs, jj * 256 + t_ * 128, jj * 256 + (t_ + 1) * 128), x_tm[:, t_, j * 128:(j + 1) * 128],
                           ident[:, :])
                for jj in range(2):
                    j = j2 + jj
                    eA, eB = ("act", "dve") if jj == 0 else ("dve", "act")
                    for (eng_, dst_, src_, v_) in ((eA, actT[:, j, 256:768], P(bA[jj]), 0),
                                                   (eB, actT[:, j, 0:256], P(bs, jj * 256, jj * 256 + 256), 1)):
                        if raw:
                            cp(eng_, dst_, src_)
                        elif eng_ == "act":
                            act(dst_, src_, AF.Identity, scale=scale_ap(j, l, v_), bias=shift_ap(j, l, v_))
                        else:
                            ts("dve", dst_, src_, scale_ap(j, l, v_), shift_ap(j, l, v_), ALU.mult, ALU.add)

        stage("a3")
        allgather_([[0, 1, 2, 3], [4, 5, 6, 7]], mg_in, mg_out, ("dram", "mg_in"), ("dram", "mg_out"))
        stage("a4")
        issue_load()
        dma("pool", Rm[:, :], rmat_d, "c_rm")
        dma("pool", BD[:, :], bd_d, "c_bd")
        issue_load()
        issue_load()
        dma("pool", maskT[:, :], mask_d, "c_mask")
        xT_phase(0, True)
        dma("sp", mo[0:96, :], mg_out.ap(), "mgo", reads=[("dram", "mg_out")])
        b3 = bank()
        tr(P(b3, 0, 96), mo[0:96, :], ident[0:96, 0:96])
        cp("dve", modfm[:, :], P(b3, 0, 96))
        ts("dve", onep[:, :], modfm[:, 32:64], 1.0, None, ALU.add)

        stage("adaln")

        def shift_ap(j, l, v):
            c = j * 4 + l * 2 + v
            return modfm[:, c:c + 1]

        def scale_ap(j, l, v):
            c = j * 4 + l * 2 + v
            return onep[:, c:c + 1]

        act(esink[:, :], esink[:, :], AF.Exp)
        stage("m1")
        for i in range(2):
            tt("dve", lamt[:, 2 * i, :], lamt[:, 2 * i, :], lamt[:, 2 * i + 1, :], ALU.mult)
            treduce_(small[:, 3 + i:4 + i], lamt[:, 2 * i, :])
        stage("m2")
        act(small[:, 3:5], small[:, 3:5], AF.Exp)
        stage("m3")
        tt("dve", small[:, 2:3], small[:, 4:5], small[:, 3:4], ALU.subtract)
        ts("dve", small[:, 2:3], small[:, 2:3], -LAM_INIT, None, ALU.add)
        ts("dve", subln_s[:, :], subln_s[:, :], 1.0 - LAM_INIT, None, ALU.mult)

        stage("misc")
        HALF = [(0, 512), (512, 768)]

        for l in range(2):
            even = (l == 0)
            NG = 4 if even else 8
            DV = 64 if even else 128
            VW = DV + 1
            NKT = 4 if even else 8
            li0 = 0 if even else 8

            def vview(reg, nt):
                return reg.m(lambda a: a[:, :, 0:NG * VW].rearrange("p t (g w) -> p t g w", w=VW))

            dma("sp", lng[:, :], ln_g_d[l].partition_broadcast(128), "c_lng")
            dma("sp", lnb[:, :], ln_b_d[l].partition_broadcast(128), "c_lnb")
            for v in range(2):
                dma("sp", gate_bc[v][:, :].m(lambda a: a.rearrange("p (j q) -> p j q", q=128)),
                    mg_out.ap()[64 + l * 2 + v:96:4, :].partition_broadcast(128), "c_gate%d" % v,
                    reads=[("dram", "mg_out")])
            stage("lc%d" % l)
            memset_("pool", vview(v_sb[:, :, :], 6).m(lambda a: a[:, :, :, DV:VW]), 1.0)
            memset_("pool", vview(vctx[:, :, :], 2).m(lambda a: a[:, :, :, DV:VW]), 1.0)

            stage("ms%d" % l)
            if l == 0:
                for j in range(8):
                    eA, eB = ("act", "dve") if j % 2 == 0 else ("dve", "act")
                    for (eng_, reg_, sa_, sh_) in ((eA, actT[:, j, 256:768], scale_ap(j, l, 0), shift_ap(j, l, 0)),
                                                   (eB, actT[:, j, 0:256], scale_ap(j, l, 1), shift_ap(j, l, 1))):
                        if eng_ == "act":
                            act(reg_, reg_, AF.Identity, scale=sa_, bias=sh_)
                        else:
                            ts("dve", reg_, reg_, sa_, sh_, ALU.mult, ALU.add)
            else:
                xT_phase(l, False)
            for t_ in range(6):
                ts("pool", x_tm[:, t_, :], x_tm[:, t_, :], ALPHA, 0.0, ALU.mult, ALU.add)
            stage("xT%d" % l)
            if even:
                for ti, (ck_, base) in enumerate(((cak, 0), (cbk, 2))):
                    for kv in range(2):
                        bb = bank()
                        for t_ in range(2):
                            tr(P(bb, t_ * 128, (t_ + 1) * 128),
                               ckl[:, t_, ti * 256 + kv * 128:ti * 256 + (kv + 1) * 128], ident[:, :])
                        cp("dve", kctx[:, base + kv, :], P(bb, 0, 256))
                for (cv_, g0) in ((cav, 0), (cbv, 2)):
                    for t_ in range(2):
                        dma("pool", vview(vctx[:, :, :], 2).m(lambda a, g0=g0, t_=t_: a[:, t_, g0:g0 + 2, 0:DV]),
                            cv_[t_ * 128:(t_ + 1) * 128, :].rearrange("p (g d) -> p g d", d=64), "vctx")
                for t_ in range(2):
                    dma("sp", ckl[:, t_, :], cck[t_ * 128:(t_ + 1) * 128, :], "ckl")
            else:
                for hh in range(8):
                    bb = bank()
                    for t_ in range(2):
                        tr(P(bb, t_ * 128, (t_ + 1) * 128), ckl[:, t_, hh * 128:(hh + 1) * 128], ident[:, :])
                    cp("dve" if hh % 2 else "act", kctx[:, hh, :], P(bb, 0, 256))
                for t_ in range(2):
                    dma("pool", vview(vctx[:, :, :], 2).m(lambda a, t_=t_: a[:, t_, :, 0:DV]),
                        ccv[t_ * 128:(t_ + 1) * 128, :].rearrange("p (g d) -> p g d", d=128), "vctx")

            stage("ctx%d" % l)
            cnt = {"fm": 0, "ko": 0, "vo": 0, "rope": 0}

            def rope(dst):
                i = cnt["rope"] % 2
                cnt["rope"] += 1
                bb = bankC()
                mm(P(bb, 0, 256), Rm[:, :], dst)
                tt("dve", rt1[i][:, :], P(bb, 0, 256), sinT[:, :], ALU.mult)
                tt("pool", rt2[i][:, :], dst, cosT[:, :], ALU.mult)
                tt("dve", dst, rt1[i][:, :], rt2[i][:, :], ALU.add)

            def sq_only(bks, i):
                act(sq[i][:, 0:512], P(bks[0]), AF.Square)
                act(sq[i][:, 512:768], P(bks[1], 0, 256), AF.Square)

            def rms_rest(i):
                c0, c1 = bankB(), bankB()
                mm(P(c0), BD[:, :], sq[i][:, 0:512])
                mm(P(c1, 0, 256), BD[:, :], sq[i][:, 512:768])
                act(rs[i][:, 0:512], P(c0), AF.Ln, scale=1.0 / 64, bias=small[:, 5:6])
                act(rs[i][:, 512:768], P(c1, 0, 256), AF.Ln, scale=1.0 / 64, bias=small[:, 5:6])
                act(rs[i][:, :], rs[i][:, :], AF.Exp, scale=-0.5)

            def fm_tile(li, ci, kind, dtile, norm, kout=None):
                w = W(li)
                i = cnt["fm"] % 3
                cnt["fm"] += 1
                bks = [bankA(), bankA()]
                for k in range(8):
                    for hb, (t0, t1) in enumerate(HALF):
                        mm(P(bks[hb], 0, t1 - t0), w[:, k, ci * 128:(ci + 1) * 128], actT[:, k, t0:t1],
                           start=(k == 0), stop=(k == 7))
                if kind == "q":
                    cp("act", qT[:, dtile, 0:512], P(bks[0]))
                    cp("dve", qT[:, dtile, 512:768], P(bks[1], 0, 256))
                    if norm:
                        sq_only(bks, i)

                    def postB_q():
                        if norm:
                            rms_rest(i)
                            stt_(qT[:, dtile, :], qT[:, dtile, :], small[:, 0:1], rs[i][:, :], ALU.mult, ALU.mult)

                    def postC_q():
                        rope(qT[:, dtile, 0:256])
                    return [postB_q, postC_q]
                cp("act", kf[i][:, 0:512], P(bks[0]))
                cp("dve", kf[i][:, 512:768], P(bks[1], 0, 256))
                if norm:
                    sq_only(bks, i)

                def postB_k():
                    if norm:
                        rms_rest(i)
                        stt_(kf[i][:, :], kf[i][:, :], small[:, 1:2], rs[i][:, :], ALU.mult, ALU.mult)

                def post_k():
                    cp("act", kT[:, dtile, :], kf[i][:, :])
                    rope(kT[:, dtile, 0:256])
                    out_d, c0, wd = kout
                    bb = bankC()
                    for t_ in range(4):
                        tr(P(bb, t_ * 128, (t_ + 1) * 128), kf[i][:, 256 + t_ * 128:256 + (t_ + 1) * 128], ident[:, :])
                    oi = cnt["ko"] % 2
                    cnt["ko"] += 1
                    cp("dve", ko[oi][:, :, 0:wd], P(bb).m(lambda a: a.rearrange("p (t c) -> p t c", c=128)[:, :, 0:wd]))
                    dma("sp", out_d[:, c0:c0 + wd].rearrange("(t p) c -> p t c", p=128), ko[oi][:, :, 0:wd],
                        "ko%d" % oi, is_out=True)
                return [postB_k, post_k]

            def tm_v(li, ncol, gsel, outs):
                w = W(li)
                for t_ in range(6):
                    bb = bank()
                    for k in range(8):
                        mm(P(bb, 0, ncol), actT[:, k, t_ * 128:(t_ + 1) * 128], w[:, k, 0:ncol],
                           start=(k == 0), stop=(k == 7))
                    g0, ng = gsel
                    cp("act", vview(v_sb[:, t_:t_ + 1, :], 1).m(lambda a: a[:, 0, g0:g0 + ng, 0:DV]),
                       P(bb, 0, ncol).m(lambda a: a.rearrange("p (g d) -> p g d", d=DV)))
                    if t_ >= 2:
                        oi = cnt["vo"] % 3
                        cnt["vo"] += 1
                        cp("dve", vo[oi][:, 0:ncol], P(bb, 0, ncol))
                        for (od, c0, s0, n) in outs:
                            dma("sp", od[(t_ - 2) * 128:(t_ - 1) * 128, c0:c0 + n], vo[oi][:, s0:s0 + n],
                                "vo%d" % oi, is_out=True)

            def tm_g(li, gc0):
                w = W(li)
                for t_ in range(6):
                    bb = bank()
                    for k in range(8):
                        mm(P(bb), actT[:, k, t_ * 128:(t_ + 1) * 128], w[:, k, 0:512], start=(k == 0), stop=(k == 7))
                    act(g_sb[:, t_, gc0:gc0 + 512], P(bb), AF.Silu)
                    if not even:
                        tt("pool", g_sb[:, t_, gc0:gc0 + 512].m(lambda a: a.rearrange("p (h d) -> p h d", d=128)),
                           g_sb[:, t_, gc0:gc0 + 512].m(lambda a: a.rearrange("p (h d) -> p h d", d=128)),
                           subln_s[:, :].m(lambda a: a.unsqueeze(1).to_broadcast([128, 4, 128])), ALU.mult)

            def kick_gather():
                gi_, go_ = g_in[l], g_out[l]
                dma("sp", gi_.ap()[0:NKT * 128, :].rearrange("(t p) n -> p t n", p=128), kT[:, 0:NKT, 0:256], "gin",
                    writes=[("dram", "gin%d" % l)])
                fV = NG * DV // 256
                for t_ in range(2):
                    dma("sp", gi_.ap()[NKT * 128 + t_ * 128 * fV:NKT * 128 + (t_ + 1) * 128 * fV, :]
                        .rearrange("(p f) c -> p (f c)", p=128, f=fV)
                        .rearrange("p (g d) -> p g d", d=DV),
                        vview(v_sb[:, 0:2, :], 2).m(lambda a, t_=t_: a[:, t_, :, 0:DV]), "gin", writes=[("dram", "gin%d" % l)])
                allgather_([[0, 1, 2, 3], [4, 5, 6, 7]], gi_, go_, ("dram", "gin%d" % l), ("dram", "gout%d" % l))


            gview = g_out[l].ap().rearrange("(r x) n -> r x n", r=4)

            pend = []

            def run_fm(*a, **kw):
                p = fm_tile(*a, **kw)
                pend.append(p)
                if len(pend) >= 2 and pend[-2][0] is not None:
                    pend[-2][0]()
                    pend[-2][0] = None
                if len(pend) >= 3 and pend[-3][1] is not None:
                    pend[-3][1]()
                    pend[-3][1] = None

            def flush_fm():
                for p in pend:
                    for j_ in range(2):
                        if p[j_] is not None:
                            p[j_]()
                            p[j_] = None
                del pend[:]

            if even:
                for ci in range(4):
                    run_fm(0, ci, "q", ci, True)
                issue_load()
                for ci in range(4):
                    od = ak_o if ci < 2 else bk_o
                    run_fm(1, ci, "k", ci, ci < 2, kout=(od, (ci % 2) * 64, 64))
                issue_load()
                for ci in range(4):
                    run_fm(2, ci, "q", 4 + ci, False)
                issue_load()
                flush_fm()
                tm_v(3, 256, (0, 4), [(av_o, 0, 0, 128), (bv_o, 0, 128, 128)])
                issue_load()
                kick_gather()
                tm_g(4, 0)
                issue_load()
                tm_g(5, 512)
                issue_load()
            else:
                for h_ in range(2):
                    for ci in range(4):
                        run_fm(8 + h_, ci, "q", 4 * h_ + ci, False)
                    issue_load()
                for h_ in range(2):
                    for ci in range(4):
                        run_fm(10 + h_, ci, "k", 4 * h_ + ci, False, kout=(ck_o, (4 * h_ + ci) * 128, 128))
                    issue_load()
                flush_fm()
                for h_ in range(2):
                    tm_v(12 + h_, 512, (4 * h_, 4), [(cv_o, 512 * h_, 0, 512)])
                    issue_load()
                kick_gather()
                for h_ in range(2):
                    tm_g(14 + h_, 512 * h_)
                    issue_load()

            stage("inproj%d" % l)
            stage("gather%d" % l)
            cntA = {"pt": 0, "u": 0}

            def attention(c, maps, units, hook=None):
                sample = (c == 0)
                nkt = 10 if sample else 2
                oab = oa[c % 2]
                ustate = {}

                def load_unit(u):
                    sl = cntA["u"] % 3
                    cntA["u"] += 1
                    ktile, vg = units[u]
                    dma("sp", kl[sl][:, :].m(lambda a: a.rearrange("p (r n) -> p r n", r=4)),
                        gview[:, ktile * 128:(ktile + 1) * 128, :].rearrange("r p n -> p r n"), "kl%d" % sl,
                        reads=[("dram", "gout%d" % l)])
                    f = NG * DV // 256
                    vsrc = gview[:, NKT * 128:NKT * 128 + 256 * f, :] \
                        .rearrange("r (t p f) c -> p r t (f c)", p=128, f=f)[:, :, :, vg * DV:(vg + 1) * DV]
                    for r in range(4):
                        dma("sp", vl[sl][:, 2 * r:2 * r + 2, 0:DV], vsrc[:, r, :, :], "vl%d" % sl,
                            reads=[("dram", "gout%d" % l)])
                    memset_("pool", vl[sl][:, :, DV:VW], 1.0)
                    ustate[u] = sl

                def kk(mp, kt):
                    b_ = mp["base"]
                    if not sample:
                        return kT[b_:b_ + 64, mp["ktile"], c * 256 + kt * 128:c * 256 + (kt + 1) * 128]
                    if kt < 8:
                        return kl[ustate[mp["unit"]]][b_:b_ + 64, kt * 128:(kt + 1) * 128]
                    return kctx[b_:b_ + 64, mp["ktile"], (kt - 8) * 128:(kt - 7) * 128]

                def vv(mp, kt):
                    g = mp["vg"]
                    if not sample:
                        return v_sb[:, 2 * c + kt, g * VW:(g + 1) * VW]
                    if kt < 8:
                        return vl[ustate[mp["unit"]]][:, kt, 0:VW]
                    return vctx[:, kt - 8, g * VW:(g + 1) * VW]

                def qk(mp):
                    sl = cntA["pt"] % 3
                    cntA["pt"] += 1
                    mp["pt"] = sl
                    b_ = mp["base"]
                    qreg = qT[b_:b_ + 64, mp["qtile"], c * 256:(c + 1) * 256]
                    for kt0 in range(0, nkt, 2):
                        bb = bankS()
                        for d_ in range(2):
                            mm(P(bb, d_ * 256, (d_ + 1) * 256), kk(mp, kt0 + d_), qreg)
                        act(pT[sl][:, kt0:kt0 + 2, :].m(lambda a: a.rearrange("p t q -> p (t q)")), P(bb), AF.Exp,
                            scale=0.125)
                    if sample and mp["mask"]:
                        tt("dve", pT[sl][:, 0:8, :].m(lambda a: a.rearrange("p t q -> p (t q)")),
                           pT[sl][:, 0:8, :].m(lambda a: a.rearrange("p t q -> p (t q)")), maskT[:, :], ALU.mult)

                def pv(mp, bb, col0):
                    sl = mp["pt"]
                    for qt in range(2):
                        for kt in range(nkt):
                            mm(P(bb, col0 + qt * VW, col0 + (qt + 1) * VW), pT[sl][:, kt, qt * 128:(qt + 1) * 128],
                               vv(mp, kt), start=(kt == 0), stop=(kt == nkt - 1))

                def post_ab(grp, bb):
                    h0 = grp[0]["sinkh"]
                    pv4 = P(bb, 0, 4 * VW).m(lambda a: a.rearrange("p (k w) -> p k w", w=VW))
                    rsum = pv4.m(lambda a: a[:, :, DV:VW].rearrange("p k o -> p (k o)"))
                    if grp[0]["sink"]:
                        tt("dve", rr[:, 0:4].m(lambda a: a.rearrange("p (g q) -> p g q", q=2)),
                           rsum.m(lambda a: a.rearrange("p (g q) -> p g q", q=2)),
                           esink[:, h0:h0 + 2].m(lambda a: a.unsqueeze(2).to_broadcast([128, 2, 2])), ALU.add)
                        recip(rr[:, 0:4], rr[:, 0:4])
                    else:
                        recip(rr[:, 0:4], rsum)
                    oc = grp[0]["ocol"]
                    for gi in range(2):
                        tt("dve", oab[:, :, oc + gi * 64:oc + (gi + 1) * 64],
                           P(bb, gi * 2 * VW, (gi + 1) * 2 * VW).m(lambda a: a.rearrange("p (q w) -> p q w", w=VW)[:, :, 0:DV]),
                           rr[:, 2 * gi:2 * gi + 2].m(lambda a: a.unsqueeze(2).to_broadcast([128, 2, DV])), ALU.mult)

                def post_c(grp, bX, bY):
                    hh = grp[0]["sinkh"]
                    X = P(bX, 0, 2 * VW).m(lambda a: a.rearrange("p (q w) -> p q w", w=VW))
                    Y = P(bY, 0, 2 * VW).m(lambda a: a.rearrange("p (q w) -> p q w", w=VW))
                    recip(rr[:, 0:2], X.m(lambda a: a[:, :, DV:VW].rearrange("p q o -> p (q o)")))
                    recip(rr[:, 2:4], Y.m(lambda a: a[:, :, DV:VW].rearrange("p q o -> p (q o)")))
                    ts("dve", rr[:, 4:6], rr[:, 2:4], small[:, 2:3], None, ALU.mult)
                    tt("dve", o1[:, :, :], X.m(lambda a: a[:, :, 0:DV]),
                       rr[:, 0:2].m(lambda a: a.unsqueeze(2).to_broadcast([128, 2, DV])), ALU.mult)
                    for qt in range(2):
                        stt_(oab[:, qt, hh * 128:(hh + 1) * 128], Y.m(lambda a, qt=qt: a[:, qt, 0:DV]),
                             rr[:, 4 + qt:5 + qt], o1[:, qt, :], ALU.mult, ALU.add)

                pend = None
                groups = [maps[i:i + 2] for i in range(0, len(maps), 2)]
                gstate = []
                loaded = set()
                nu = len(units)
                if sample:
                    load_unit(0)
                    loaded.add(0)
                flat = []
                for g in groups:
                    for mp in g:
                        flat.append((mp, g))
                pvbank = {}

                def do_pv(mp, g):
                    gid = id(g)
                    if even:
                        if gid not in pvbank:
                            pvbank[gid] = bankPV()
                        gi = 0 if mp is g[0] else 1
                        pv(mp, pvbank[gid], gi * 2 * VW)
                        if mp is g[-1]:
                            post_ab(g, pvbank[gid])
                    else:
                        if gid not in pvbank:
                            pvbank[gid] = [bankPV(), bankPV()]
                        gi = 0 if mp is g[0] else 1
                        pv(mp, pvbank[gid][gi], 0)
                        if mp is g[-1]:
                            post_c(g, pvbank[gid][0], pvbank[gid][1])

                for idx, (mp, g) in enumerate(flat):
                    if idx == 2 and hook is not None:
                        hook()
                    if sample:
                        u = mp["unit"]
                        if u + 1 < nu and (u + 1) not in loaded:
                            load_unit(u + 1)
                            loaded.add(u + 1)
                    qk(mp)
                    if pend is not None:
                        do_pv(*pend)
                    pend = (mp, g)
                do_pv(*pend)

                if not even:
                    slq = cntA["pt"] % 3
                    cntA["pt"] += 1
                    sqb = pT[slq][:, 0:8, :]
                    act(sqb.m(lambda a: a.rearrange("p t q -> p (t q)")),
                        oab[:, :, :].m(lambda a: a.rearrange("p q d -> p (q d)")), AF.Square)
                    treduce_(ssb[:, 0:16], sqb.m(lambda a: a.rearrange("p t q -> p (t q)").rearrange("p (g d) -> p g d", d=128)))
                    act(ssb[:, 0:16], ssb[:, 0:16], AF.Ln, scale=1.0 / 128, bias=small[:, 5:6])
                    act(ssb[:, 0:16], ssb[:, 0:16], AF.Exp, scale=-0.5)
                    tt("dve", oab[:, :, :].m(lambda a: a.rearrange("p q (h d) -> p (q h) d", d=128)),
                       oab[:, :, :].m(lambda a: a.rearrange("p q (h d) -> p (q h) d", d=128)),
                       ssb[:, 0:16].m(lambda a: a.unsqueeze(2).to_broadcast([128, 16, 128])), ALU.mult)
                tt("pool", oab[:, 0, :], oab[:, 0, :], g_sb[:, 2 * c, :], ALU.mult)
                tt("dve", oab[:, 1, :], oab[:, 1, :], g_sb[:, 2 * c + 1, :], ALU.mult)

                def epi_pe(bank_fn=bank):
                    for qt in range(2):
                        t_ = 2 * c + qt
                        for hf in range(2):
                            bb = bank_fn()
                            for j in range(4):
                                tr(P(bb, j * 128, (j + 1) * 128), oab[:, qt, (4 * hf + j) * 128:(4 * hf + j + 1) * 128],
                                   ident[:, :])
                            cp("act" if hf == 0 else "dve", actT[:, 4 * hf:4 * hf + 4, t_ * 128:(t_ + 1) * 128],
                               P(bb).m(lambda a: a.rearrange("p (j q) -> p j q", q=128)))
                return epi_pe

            def make_maps():
                maps, units = [], []
                if even:
                    for typ in range(2):
                        for kv in range(2):
                            units.append((typ * 2 + kv, typ * 2 + kv))
                            for hq in range(4):
                                h = kv * 4 + hq
                                maps.append(dict(ktile=typ * 2 + kv, qtile=typ * 4 + h // 2, base=(h % 2) * 64,
                                                 vg=typ * 2 + kv, unit=typ * 2 + kv, sinkh=h, sink=(typ == 1),
                                                 mask=(typ == 1), ocol=typ * 512 + (h - h % 2) * 64))
                else:
                    for hh in range(8):
                        units.append((hh, hh))
                        for m_ in range(2):
                            maps.append(dict(ktile=hh, qtile=hh, base=m_ * 64, vg=hh, unit=hh, sinkh=hh, sink=False,
                                             mask=False, ocol=hh * 128))
                return maps, units

            wl = [6, 7] if even else [16, 17]
            yic = {"n": 0}

            def outproj_ln(tiles, bank_fn=bank, split=False):
              for t_ in tiles:
                  v = 1 if t_ < 2 else 0
                  yi = yic["n"]
                  y_ = yt[yi % 2]
                  stt = stts[yi % 2]
                  yic["n"] += 1
                  bks = [bank_fn(), bank_fn()]
                  for hf in range(2):
                      for k in range(8):
                          mm(P(bks[hf]), actT[:, k, t_ * 128:(t_ + 1) * 128], W(wl[hf])[:, k, 0:512],
                             start=(k == 0), stop=(k == 7))
                  for hf in range(2):
                      tt("dve", y_[:, hf * 512:(hf + 1) * 512], P(bks[hf]), gate_bc[v][:, hf * 512:(hf + 1) * 512], ALU.mult)
                  if split:
                      tt("pool", y_[:, 0:512], y_[:, 0:512], x_tm[:, t_, 0:512], ALU.add)
                      tt("dve", y_[:, 512:1024], y_[:, 512:1024], x_tm[:, t_, 512:1024], ALU.add)
                  else:
                      tt("pool", y_[:, :], y_[:, :], x_tm[:, t_, :], ALU.add)
                  for hf in range(2):
                      bnstats_(stt[:, 32 + hf * 6:38 + hf * 6], y_[:, hf * 512:(hf + 1) * 512])
                  bnaggr_(stt[:, 0:2], stt[:, 32:44])
                  act(stt[:, 2:3], stt[:, 1:2], AF.Ln, bias=small[:, 5:6])
                  act(stt[:, 2:3], stt[:, 2:3], AF.Exp, scale=-0.5)
                  ts("dve", stt[:, 3:4], stt[:, 0:1], -1.0, stt[:, 2:3], ALU.mult, ALU.mult)
                  act(y_[:, :], y_[:, :], AF.Identity, scale=stt[:, 2:3], bias=stt[:, 3:4])
                  tt("dve", y_[:, :], y_[:, :], lng[:, :], ALU.mult)
                  if split:
                      tt("pool", x_tm[:, t_, 0:512], y_[:, 0:512], lnb[:, 0:512], ALU.add)
                      tt("dve", x_tm[:, t_, 512:1024], y_[:, 512:1024], lnb[:, 512:1024], ALU.add)
                  else:
                      tt("pool", x_tm[:, t_, :], y_[:, :], lnb[:, :], ALU.add)
                  if l == 1:
                      dma("pool", y_o[t_ * 128:(t_ + 1) * 128, :], x_tm[:, t_, :], "yo%d" % (t_ % 2), is_out=True)
                  elif DBG:
                      dma("sp", dbg_o[t_ * 128:(t_ + 1) * 128, :], x_tm[:, t_, :], "yo%d" % (t_ % 2), is_out=True)

            maps, units = make_maps()
            epi1 = attention(1, maps, units)
            maps, units = make_maps()
            epi2 = attention(2, maps, units)
            epi1()
            outproj_ln((2, 3), split=True)

            def mid_hook():
                epi2(bankS)
                outproj_ln((4, 5), bankS)
            maps, units = make_maps()
            epi0 = attention(0, maps, units, hook=mid_hook)
            stage("attn%d_0" % l)
            epi0()
            outproj_ln((0, 1), split=True)
            if even:
                for _ in range(2):
                    issue_load()
            else:
                pass


    try:
        body()
    except _Stop:
        pass

    fin = S.op("sp", None, reads=[], writes=[])
    for o in S.out_ops:
        fin.dwaits[o.key] = 16 * S.dcnt[o.key]
    S.finalize()

    sem_names = {}
    import contextlib
    with contextlib.ExitStack() as es:
        csem = {e: es.enter_context(nc.semaphore("c_" + e)) for e in ("pe", "act", "dve", "pool")}
        dsem = {k: es.enter_context(nc.semaphore("d_" + k)) for k in S.dcnt}
        ccsem = [es.enter_context(nc.semaphore("cc%d" % i)) for i in range(S.ncc)]
        block = es.enter_context(nc.Block())

        def run(engname, e):
            known = {}

            def wait(sem, val, tag):
                if known.get(tag, 0) < val:
                    e.wait_ge(sem, val)
                    known[tag] = val

            for o in S.q[engname]:
                need = {}
                for d_ in o.deps:
                    if d_.kind == "c":
                        tag = ("c", d_.eng)
                        need[tag] = max(need.get(tag, 0), d_.sig)
                    elif d_.kind == "cc":
                        need[("cc", d_.ccid)] = 1
                for k_, v_ in o.dwaits.items():
                    need[("d", k_)] = max(need.get(("d", k_), 0), v_)
                for tag, val in need.items():
                    if tag[0] == "c":
                        wait(csem[tag[1]], val, tag)
                    elif tag[0] == "d":
                        wait(dsem[tag[1]], val, tag)
                    else:
                        wait(ccsem[tag[1]], val, tag)
                if o.fn is None:
                    continue
                ins = o.fn(e)
                if o.kind == "c":
                    if o.need_inc:
                        ins.then_inc(csem[engname], 1)
                elif o.kind == "d":
                    ins.then_inc(dsem[o.key], 16)
                else:
                    ins.then_inc(ccsem[o.ccid])

        @block.tensor
        def _(e):
            run("pe", e)

        @block.scalar
        def _(e):
            run("act", e)

        @block.vector
        def _(e):
            run("dve", e)

        @block.gpsimd
        def _(e):
            run("pool", e)

        @block.sync
        def _(e):
            run("sp", e)
    return nc


def _consts(r):
    ident = np.eye(128, dtype=np.float32)
    rmat = np.zeros((128, 128), np.float32)
    for m in range(128):
        if (m % 32) < 16:
            rmat[m + 16, m] = -1.0
        else:
            rmat[m - 16, m] = 1.0
    bd = np.zeros((128, 128), np.float32)
    bd[0:64, 0:64] = 1.0
    bd[64:128, 64:128] = 1.0
    t = 256 * r + np.arange(256)
    row = (t // 64).astype(np.float64)
    col = (t % 64).astype(np.float64)
    freqs = (10000.0 ** (-(np.arange(16, dtype=np.float32) / np.float32(16)))).astype(np.float64)
    ang = np.zeros((128, 256))
    for p in range(128):
        d = p % 64
        f = freqs[d % 16]
        ang[p] = (row if d < 32 else col) * f
    cosT = np.cos(ang).astype(np.float32)
    sinT = np.sin(ang).astype(np.float32)
    j = (np.arange(8)[None, :, None] * 128 + np.arange(128)[:, None, None])
    i = (256 * r + np.arange(256))[None, None, :]
    mask = (np.abs(j - i) <= 128).astype(np.float32).reshape(128, 2048)
    return dict(ident=ident, rmat=rmat, bdmat=bd, cosT=cosT, sinT=sinT, maskd=mask)


_NC_CACHE = {}


def make_in_maps(x_prompt, x_sample, cache_a_k, cache_a_v, cache_b_k, cache_b_v, cache_c_k, cache_c_v,
                 c, c_ctx, w_mod, b_mod, ln_g, ln_b, w_in_even, w_out_even, q_norm_a, k_norm_a, sink_b,
                 w_in_odd, w_out_odd, lambda_q1, lambda_k1, lambda_q2, lambda_k2, subln_c):
    f = lambda a: np.ascontiguousarray(np.asarray(a, dtype=np.float32))
    x_prompt, x_sample = f(x_prompt), f(x_sample)
    in_maps = []
    for i in range(8):
        b, r = i // 4, i % 4
        m = dict(
            xs=np.concatenate([x_sample[b, 256 * r:256 * (r + 1)], x_prompt[2 * i], x_prompt[2 * i + 1]], 0),
            cak=f(cache_a_k)[b, 0].reshape(256, 128), cav=f(cache_a_v)[b, 0].reshape(256, 128),
            cbk=f(cache_b_k)[b, 0].reshape(256, 128), cbv=f(cache_b_v)[b, 0].reshape(256, 128),
            cck=f(cache_c_k)[b, 0].reshape(256, 1024), ccv=f(cache_c_v)[b, 0].reshape(256, 1024),
            cvec=np.stack([f(c_ctx), f(c)[b]], 0),
            wmod_sh=f(f(w_mod)[:, :, 768 * r:768 * (r + 1)]), bmod_sh=f(f(b_mod)[:, 768 * r:768 * (r + 1)]),
            ln_g=f(ln_g), ln_b=f(ln_b),
            w_in_e=f(w_in_even)[0], w_out_e=f(w_out_even)[0], w_in_o=f(w_in_odd)[0], w_out_o=f(w_out_odd)[0],
            qn=f(q_norm_a)[0], kn=f(k_norm_a)[0], sink=f(sink_b)[0],
            lq1=f(lambda_q1)[0], lk1=f(lambda_k1)[0], lq2=f(lambda_q2)[0], lk2=f(lambda_k2)[0],
            subln=f(subln_c)[0],
        )
        m.update(_consts(r))
        in_maps.append({k: np.ascontiguousarray(v) for k, v in m.items()})
    return in_maps


def assemble(R):
    y_prompt = np.zeros((16, 256, 1024), np.float32)
    y_sample = np.zeros((2, 1024, 1024), np.float32)
    nak = np.zeros((16, 1, 256, 2, 64), np.float32)
    nav, nbk, nbv = np.zeros_like(nak), np.zeros_like(nak), np.zeros_like(nak)
    nck = np.zeros((16, 1, 256, 8, 128), np.float32)
    ncv = np.zeros_like(nck)
    for i in range(8):
        b, r = i // 4, i % 4
        y = R[i]["y"]
        y_sample[b, 256 * r:256 * (r + 1)] = y[0:256]
        for s in range(2):
            y_prompt[2 * i + s] = y[256 * (s + 1):256 * (s + 2)]
            sl = slice(256 * s, 256 * (s + 1))
            nak[2 * i + s, 0] = R[i]["ak"][sl].reshape(256, 2, 64)
            nav[2 * i + s, 0] = R[i]["av"][sl].reshape(256, 2, 64)
            nbk[2 * i + s, 0] = R[i]["bk"][sl].reshape(256, 2, 64)
            nbv[2 * i + s, 0] = R[i]["bv"][sl].reshape(256, 2, 64)
            nck[2 * i + s, 0] = R[i]["ck"][sl].reshape(256, 8, 128)
            ncv[2 * i + s, 0] = R[i]["cv"][sl].reshape(256, 8, 128)
    return (y_prompt, y_sample, nak, nav, nbk, nbv, nck, ncv)


def kernel(**inputs):
    in_maps = make_in_maps(**inputs)
    if "nc" not in _NC_CACHE:
        _NC_CACHE["nc"] = build_nc()
    res = run_bass_kernel_spmd(_NC_CACHE["nc"], in_maps, core_ids=list(range(8)))
    return assemble(res.results)
```

```python
import math
import numpy as np
import concourse.bass as bass
import concourse.mybir as mybir
from concourse.bass_utils import run_bass_kernel_spmd

F32 = mybir.dt.float32
BF16 = mybir.dt.bfloat16
AF = mybir.ActivationFunctionType
ALU = mybir.AluOpType
AX = mybir.AxisListType

D = 1024
EPS = 1e-6
ALPHA = (2 * 2) ** 0.25
LAM_INIT = 0.8 - 0.6 * math.exp(-0.3 * 1)
BLK = 256
ENGS = ["pe", "act", "dve", "pool", "sp"]


class Reg:
    def __init__(s, ap, toks):
        s.ap, s.toks = ap, toks

    def m(s, f):
        return Reg(f(s.ap), s.toks)


class T:
    def __init__(s, nc, name, shape, dt, arena=None, off=0, tok=None):
        s.shape = list(shape)
        s.dt = dt
        s.isz = 4 if dt == F32 else 2
        n = int(np.prod(shape))
        if arena is None:
            s.h = nc.alloc_sbuf_tensor("sb_" + name, [128, n], dt)
            base = s.h[:, :]
            s.tok = name
            s.boff = 0
        else:
            a = arena[:, off // 2:(off + n * s.isz) // 2]
            base = a.bitcast(F32) if dt == F32 else a
            s.tok = tok
            s.boff = off
        if len(shape) == 1:
            s.full = base
        elif len(shape) == 2:
            s.full = base.rearrange("p (a b) -> p a b", b=shape[1])
        else:
            s.full = base.rearrange("p (a b c) -> p a b c", b=shape[1], c=shape[2])
        st = [1]
        for d in reversed(shape[1:]):
            st.insert(0, st[0] * d)
        s.strides = st

    def __getitem__(s, idx):
        if not isinstance(idx, tuple):
            idx = (idx,)
        ap = s.full[idx]
        lo = hi = 0
        for d, (n, st) in enumerate(zip(s.shape, s.strides)):
            i = idx[d + 1] if d + 1 < len(idx) else slice(None)
            if isinstance(i, int):
                a, b = i, i + 1
            else:
                a = i.start or 0
                b = n if i.stop is None else i.stop
            lo += a * st
            hi += (b - 1) * st
        lo_b = s.boff + lo * s.isz
        hi_b = s.boff + (hi + 1) * s.isz
        return Reg(ap, [(s.tok, k) for k in range(lo_b // BLK, (hi_b - 1) // BLK + 1)])


class Op:
    __slots__ = ("eng", "fn", "kind", "key", "deps", "dwaits", "need_inc", "sig", "ccid")

    def __init__(s, eng, fn, kind, key):
        s.eng, s.fn, s.kind, s.key = eng, fn, kind, key
        s.deps = []
        s.dwaits = {}
        s.need_inc = False
        s.sig = 0
        s.ccid = None


class Sched:
    def __init__(s):
        s.q = {e: [] for e in ENGS}
        s.W = {}
        s.R = {}
        s.dcnt = {}
        s.ncc = 0
        s.out_ops = []

    @staticmethod
    def _toks(lst):
        out = []
        for r in lst:
            if r is None:
                continue
            if isinstance(r, Reg):
                out.extend(r.toks)
            elif isinstance(r, (tuple, str)):
                out.append(r)
            else:
                raise TypeError(type(r))
        return out

    def _dep(s, o, w):
        if w is o:
            return
        if w.kind == "d":
            if o.kind == "d" and o.key == w.key:
                return
            o.dwaits[w.key] = 16 * s.dcnt[w.key]
            t = ("sem", w.key)
            s.R.setdefault(t, []).append(o)
        else:
            if w.eng == "pe" and o.eng == "pe" and o.kind == "c":
                return
            o.deps.append(w)
            w.need_inc = True

    def op(s, eng, fn, reads=(), writes=(), kind="c", key=None):
        o = Op(eng, fn, kind, key)
        rt = s._toks(reads)
        wt = s._toks(writes)
        if eng != "pe":
            wt = wt + [t for t in rt if isinstance(t, tuple) and t[0] == "ps" and t not in wt]
        seen = set()
        for t in rt:
            for w in s.W.get(t, ()):
                if id(w) not in seen:
                    seen.add(id(w))
                    s._dep(o, w)
        for t in wt:
            for w in s.W.get(t, ()):
                if id(w) not in seen:
                    seen.add(id(w))
                    s._dep(o, w)
            for w in s.R.get(t, ()):
                if id(w) not in seen:
                    seen.add(id(w))
                    s._dep(o, w)
        if kind == "d":
            t = ("sem", key)
            for w in s.R.get(t, ()):
                if id(w) not in seen:
                    seen.add(id(w))
                    s._dep(o, w)
            s.dcnt[key] = s.dcnt.get(key, 0) + 1
            s.R[t] = []
        if kind == "cc":
            o.ccid = s.ncc
            s.ncc += 1
        for t in rt:
            s.R.setdefault(t, []).append(o)
        for t in wt:
            s.W[t] = [o]
            s.R[t] = []
        s.q[eng].append(o)
        return o

    def finalize(s):
        for e in ENGS:
            c = 0
            for o in s.q[e]:
                if o.kind == "c" and o.need_inc:
                    c += 1
                    o.sig = c


class _Stop(Exception):
    pass


def build_nc(stop=None):
    nc = bass.Bass("TRN2", target_bir_lowering=False)
    S = Sched()

    def stage(name):
        if stop == name:
            raise _Stop()

    def din(name, shape, dt=F32):
        return nc.dram_tensor(name, shape, dt, kind="ExternalInput").ap()

    def dout(name, shape):
        return nc.dram_tensor(name, shape, F32, kind="ExternalOutput").ap()

    xs = din("xs", [768, D])
    cak, cav, cbk, cbv = (din(n, [256, 128]) for n in ("cak", "cav", "cbk", "cbv"))
    cck, ccv = din("cck", [256, D]), din("ccv", [256, D])
    cvec = din("cvec", [2, D])
    wmod_sh = din("wmod_sh", [2, D, 768])
    bmod_sh = din("bmod_sh", [2, 768])
    ln_g_d, ln_b_d = din("ln_g", [2, D]), din("ln_b", [2, D])
    w_in_e, w_out_e = din("w_in_e", [D, 2560]), din("w_out_e", [D, D])
    w_in_o, w_out_o = din("w_in_o", [D, 4096]), din("w_out_o", [D, D])
    qn_d, kn_d = din("qn", [64]), din("kn", [64])
    sink_d = din("sink", [8])
    lq1_d, lk1_d, lq2_d, lk2_d = (din(n, [64]) for n in ("lq1", "lk1", "lq2", "lk2"))
    subln_d = din("subln", [128])
    ident_d = din("ident", [128, 128])
    rmat_d, bd_d = din("rmat", [128, 128]), din("bdmat", [128, 128])
    cos_d, sin_d = din("cosT", [128, 256]), din("sinT", [128, 256])
    mask_d = din("maskd", [128, 2048])

    y_o = dout("y", [768, D])
    ak_o, av_o, bk_o, bv_o = (dout(n, [512, 128]) for n in ("ak", "av", "bk", "bv"))
    ck_o, cv_o = dout("ck", [512, D]), dout("cv", [512, D])
    import os
    DBG = bool(os.environ.get("KDBG"))
    dbg_o = dout("dbg", [768, D]) if DBG else None

    mg_in = nc.dram_tensor("mg_in", [24, 128], F32)
    mg_out = nc.dram_tensor("mg_out", [96, 128], F32)
    g_in = [nc.dram_tensor("g0_in", [768, 256], BF16), nc.dram_tensor("g1_in", [2048, 256], BF16)]
    g_out = [nc.dram_tensor("g0_out", [4 * 768, 256], BF16), nc.dram_tensor("g1_out", [4 * 2048, 256], BF16)]
    GROWS = [768, 2048]

    x_tm = T(nc, "x_tm", [6, D], F32)
    actT = T(nc, "actT", [8, 768], BF16)
    qT = T(nc, "qT", [8, 768], BF16)
    kT = T(nc, "kT", [8, 768], BF16)
    v_sb = T(nc, "v_sb", [6, 1032], BF16)
    g_sb = T(nc, "g_sb", [6, D], BF16)
    wring = [T(nc, "wr%d" % i, [8, 512], BF16) for i in range(3)]
    ident = T(nc, "ident", [128], F32)
    Rm = T(nc, "Rm", [128], BF16)
    BD = T(nc, "BD", [128], BF16)
    cosT = T(nc, "cosT", [256], F32)
    sinT = T(nc, "sinT", [256], F32)
    maskT = T(nc, "maskT", [2048], BF16)
    modfm = T(nc, "modfm", [96], F32)
    onep = T(nc, "onep", [32], F32)
    small = T(nc, "small", [64], F32)
    esink = T(nc, "esink", [8], F32)
    subln_s = T(nc, "subln_s", [128], F32)
    lamt = T(nc, "lamt", [4, 64], F32)
    kctx = T(nc, "kctx", [8, 256], BF16)
    vctx = T(nc, "vctx", [2, 1032], BF16)
    gate_bc = [T(nc, "gate_bc%d" % v, [D], F32) for v in range(2)]
    lng = T(nc, "lng", [D], F32)
    lnb = T(nc, "lnb", [D], F32)
    ckl = T(nc, "ckl", [2, D], F32)
    stk = T(nc, "stk", [128], F32)
    scT = T(nc, "scT", [16], BF16)
    bmT = T(nc, "bmT", [12], F32)
    resm = T(nc, "resm", [24], F32)
    resT = T(nc, "resT", [128], F32)
    mo = T(nc, "mo", [128], F32)
    wm = [T(nc, "wm0", [8, 768], BF16, arena=qT.h, off=0, tok="qT"),
          T(nc, "wm1", [8, 768], BF16, arena=kT.h, off=0, tok="kT")]
    ARN = 46 * 1024
    arena = nc.alloc_sbuf_tensor("arena", [128, ARN // 2], BF16)

    def AT(name, shape, dt, off):
        t = T(nc, name, shape, dt, arena=arena, off=off, tok="ARENA")
        assert off + int(np.prod(shape)) * t.isz <= ARN, name
        return t

    K = 1024
    kf = [AT("kf%d" % i, [768], F32, i * 3 * K) for i in range(3)]
    ko = [AT("ko%d" % i, [4, 128], F32, 9 * K + i * 2 * K) for i in range(2)]
    vo = [AT("vo%d" % i, [512], F32, 13 * K + i * 2 * K) for i in range(3)]
    sq = [AT("sq%d" % i, [768], BF16, 19 * K + i * 1536) for i in range(3)]
    rs = [AT("rs%d" % i, [768], F32, 24 * K + i * 3 * K) for i in range(3)]
    rt1 = [AT("rt1%d" % i, [256], F32, 33 * K + i * K) for i in range(2)]
    rt2 = [AT("rt2%d" % i, [256], F32, 35 * K + i * K) for i in range(2)]
    kl = [AT("kl%d" % i, [1024], BF16, i * 2 * K) for i in range(3)]
    vl = [AT("vl%d" % i, [8, 129], BF16, 6 * K + i * 2112) for i in range(3)]
    pT = [AT("pT%d" % i, [10, 256], BF16, 13 * K + i * 5 * K) for i in range(3)]
    oa = [AT("oa%d" % i, [2, D], F32, 28 * K + i * 8 * K) for i in range(2)]
    rr = AT("rr", [64], F32, 44 * K)
    o1 = AT("o1", [2, 128], F32, 44 * K + 512)
    ssb = AT("ssb", [64], F32, 45 * K + 512)
    yt = [T(nc, "yt%d" % i, [D], F32) for i in range(2)]
    stts = [T(nc, "stt%d" % i, [64], F32) for i in range(2)]

    ps = nc.alloc_psum_tensor("ps", [128, 8, 512], F32)

    def P(b, lo=0, hi=512, p0=0, p1=128):
        return Reg(ps[p0:p1, b, lo:hi], [("ps", b)])

    rrb = {"n": 0, "s": 0, "pv": 0}

    def bank():
        b = rrb["n"] % 8
        rrb["n"] += 1
        return b

    def bankS():
        b = rrb["s"] % 4
        rrb["s"] += 1
        return b

    def bankA():
        b = rrb.get("a", 0) % 4
        rrb["a"] = rrb.get("a", 0) + 1
        return b

    def bankB():
        b = 4 + rrb.get("b", 0) % 2
        rrb["b"] = rrb.get("b", 0) + 1
        return b

    def bankC():
        b = 6 + rrb.get("c", 0) % 2
        rrb["c"] = rrb.get("c", 0) + 1
        return b

    def bankPV():
        b = 4 + rrb["pv"] % 4
        rrb["pv"] += 1
        return b

    def mm(out, lhsT, rhs, start=True, stop=True):
        S.op("pe", lambda e: e.matmul(out.ap, lhsT.ap, rhs.ap, start=start, stop=stop),
             reads=[lhsT, rhs], writes=[out])

    def tr(out, in_, idr):
        S.op("pe", lambda e: e.transpose(out.ap, in_.ap, idr.ap), reads=[in_, idr], writes=[out])

    def act(out, in_, func, scale=1.0, bias=0.0, eng="act"):
        rd = [in_]
        sc = scale.ap if isinstance(scale, Reg) else scale
        bi = bias.ap if isinstance(bias, Reg) else bias
        if isinstance(scale, Reg):
            rd.append(scale)
        if isinstance(bias, Reg):
            rd.append(bias)
        S.op("act", lambda e: e.activation(out.ap, in_.ap, func, bias=bi, scale=sc), reads=rd, writes=[out])

    def tt(eng, out, in0, in1, op):
        S.op(eng, lambda e: e.tensor_tensor(out.ap, in0.ap, in1.ap, op), reads=[in0, in1], writes=[out])

    def ts(eng, out, in0, s1, s2, op0, op1=None):
        rd = [in0]
        a1 = s1.ap if isinstance(s1, Reg) else s1
        a2 = s2.ap if isinstance(s2, Reg) else s2
        if isinstance(s1, Reg):
            rd.append(s1)
        if isinstance(s2, Reg):
            rd.append(s2)
        if op1 is None:
            S.op(eng, lambda e: e.tensor_scalar(out.ap, in0.ap, a1, None, op0), reads=rd, writes=[out])
        else:
            S.op(eng, lambda e: e.tensor_scalar(out.ap, in0.ap, a1, a2, op0, op1), reads=rd, writes=[out])

    def stt_(out, in0, sc, in1, op0, op1):
        rd = [in0, in1]
        a = sc.ap if isinstance(sc, Reg) else sc
        if isinstance(sc, Reg):
            rd.append(sc)
        S.op("dve", lambda e: e.scalar_tensor_tensor(out.ap, in0.ap, a, in1.ap, op0, op1), reads=rd, writes=[out])

    def cp(eng, out, in_):
        if eng == "act":
            S.op("act", lambda e: e.copy(out.ap, in_.ap), reads=[in_], writes=[out])
        else:
            S.op(eng, lambda e: e.tensor_copy(out.ap, in_.ap), reads=[in_], writes=[out])

    def dma(q, out, in_, key, reads=(), writes=(), is_out=False):
        oa_ = out.ap if isinstance(out, Reg) else out
        ia_ = in_.ap if isinstance(in_, Reg) else in_
        rd = list(reads) + ([in_] if isinstance(in_, Reg) else [])
        wr = list(writes) + ([out] if isinstance(out, Reg) else [])
        o = S.op(q, lambda e: e.dma_start(out=oa_, in_=ia_), reads=rd, writes=wr, kind="d", key=key)
        if is_out:
            S.out_ops.append(o)
        return o

    def recip(out, in_):
        S.op("dve", lambda e: e.reciprocal(out.ap, in_.ap), reads=[in_], writes=[out])

    def memset_(eng, reg, val):
        S.op(eng, lambda e: e.memset(reg.ap, val), writes=[reg])

    def bnstats_(out, in_):
        S.op("dve", lambda e: e.bn_stats(out.ap, in_.ap), reads=[in_], writes=[out])

    def bnaggr_(out, in_):
        S.op("dve", lambda e: e.bn_aggr(out.ap, in_.ap), reads=[in_], writes=[out])

    def treduce_(out, in_):
        S.op("dve", lambda e: e.tensor_reduce(out=out.ap, in_=in_.ap, axis=AX.X, op=ALU.add), reads=[in_], writes=[out])

    def allgather_(groups, din_, dout_, rtok, wtok):
        S.op("pool", lambda e: e.collective_compute("AllGather", ALU.bypass, replica_groups=groups,
                                                    ins=[din_.ap().opt()], outs=[dout_.ap().opt()]),
             reads=[rtok], writes=[wtok], kind="cc")

    memset_("dve", small[:, 5:6], EPS)

    def body():
        dma("sp", ident[:, :], ident_d, "c_id")
        stk_toks = [("stkp", n) for n in range(15)]
        S.op("pool", lambda e: e.memset(stk.full[:, :], 0.0), writes=stk_toks)
        n_ = 0
        for v in range(2):
            dma("sp", Reg(stk.full[8 * v:8 * v + 8, :], [stk_toks[n_]]), cvec[v].rearrange("(k p) -> k p", p=128), "c_stk%d" % n_)
            n_ += 1
        for t in range(6):
            for l in range(2):
                r = 16 + t * 2 + l
                dma("sp", Reg(stk.full[r:r + 1, :], [stk_toks[n_]]), bmod_sh[l:l + 1, t * 128:(t + 1) * 128], "c_stk%d" % n_)
                n_ += 1
        stk_all = Reg(stk.full[0:28, :], stk_toks)
        for l in range(2):
            dma("pool", wm[l][:, :, :], wmod_sh[l].rearrange("(k p) n -> p k n", p=128), "c_wm%d" % l)
        for tt_i in range(6):
            dma("sp", x_tm[:, tt_i, :], xs[tt_i * 128:(tt_i + 1) * 128, :], "x%d" % tt_i)
        for ti, ck_ in enumerate((cak, cbk)):
            cv4 = ckl[:, :, ti * 256:(ti + 1) * 256].m(lambda a: a.rearrange("p t (g d) -> p t g d", d=64))
            for kv in range(2):
                for dup in range(2):
                    g_ = 2 * kv + dup
                    dma("sp", cv4.m(lambda a, g_=g_: a[:, :, g_, :]),
                        ck_[:, kv * 64:(kv + 1) * 64].rearrange("(t p) d -> p t d", p=128), "ckl")
        dma("sp", cosT[:, :], cos_d, "c_cos")
        dma("sp", sinT[:, :], sin_d, "c_sin")
        dma("sp", small[0:64, 0:1], qn_d.rearrange("(p o) -> p o", o=1), "c_sm")
        dma("sp", small[64:128, 0:1], qn_d.rearrange("(p o) -> p o", o=1), "c_sm")
        dma("sp", small[0:64, 1:2], kn_d.rearrange("(p o) -> p o", o=1), "c_sm")
        dma("sp", small[64:128, 1:2], kn_d.rearrange("(p o) -> p o", o=1), "c_sm")
        dma("sp", esink[:, :], sink_d.partition_broadcast(128), "c_sink")
        for i, d_ in enumerate((lq1_d, lk1_d, lq2_d, lk2_d)):
            dma("sp", lamt[:, i, :], d_.partition_broadcast(128), "c_lam")
        dma("sp", subln_s[:, :], subln_d.partition_broadcast(128), "c_subln")

        stage("cm1")
        loads = []

        def wsrc(w, c0, n):
            return w[:, c0:c0 + n].rearrange("(k p) n -> p k n", p=128)

        kslot = []
        for base in (512, 1792):
            for kv in range(2):
                for dup in range(2):
                    kslot.append((len(kslot) * 64, wsrc(w_in_e, base + kv * 64, 64), 64))
        loads.append([(0, wsrc(w_in_e, 0, 512), 512)])
        loads.append(kslot)
        loads.append([(0, wsrc(w_in_e, 1280, 512), 512)])
        loads.append([(0, wsrc(w_in_e, 640, 128), 128), (128, wsrc(w_in_e, 1920, 128), 128)])
        loads.append([(0, wsrc(w_in_e, 768, 512), 512)])
        loads.append([(0, wsrc(w_in_e, 2048, 512), 512)])
        loads.append([(0, wsrc(w_out_e, 0, 512), 512)])
        loads.append([(0, wsrc(w_out_e, 512, 512), 512)])
        for i in range(8):
            loads.append([(0, wsrc(w_in_o, 512 * i, 512), 512)])
        loads.append([(0, wsrc(w_out_o, 0, 512), 512)])
        loads.append([(0, wsrc(w_out_o, 512, 512), 512)])
        wstate = {"next": 0}

        def issue_load():
            i = wstate["next"]
            if i >= len(loads):
                return
            wstate["next"] += 1
            sl = i % 3
            for (c0, src, n) in loads[i]:
                dma("pool", wring[sl][:, :, c0:c0 + n], src, "w%d" % sl)

        def W(i):
            return wring[i % 3]

        stage("c0")
        b0 = bank()
        tr(P(b0, 0, 28), stk_all, ident[0:28, 0:28])
        act(scT[:, :].m(lambda a: a.rearrange("p (k v) -> p v k", v=2)),
            P(b0, 0, 16).m(lambda a: a.rearrange("p (v k) -> p v k", v=2)), AF.Silu)
        cp("dve", bmT[:, :], P(b0, 16, 28))
        stage("a1")
        b1 = bank()
        for l in range(2):
            for t in range(6):
                c0 = t * 4 + l * 2
                for k in range(8):
                    mm(P(b1, c0, c0 + 2), wm[l][:, k, t * 128:(t + 1) * 128], scT[:, 2 * k:2 * k + 2],
                       start=(k == 0), stop=(k == 7))
        tt("dve", resm[:, :].m(lambda a: a.rearrange("p (g v) -> p g v", v=2)),
           P(b1, 0, 24).m(lambda a: a.rearrange("p (g v) -> p g v", v=2)),
           bmT[:, :].m(lambda a: a.unsqueeze(2).to_broadcast([128, 12, 2])), ALU.add)
        stage("a2")
        b2 = bank()
        tr(P(b2, 0, 128, 0, 24), resm[:, :], ident[:, :])
        cp("dve", resT[0:24, :], P(b2, 0, 128, 0, 24))
        dma("sp", mg_in.ap(), resT[0:24, :], "mgi", writes=[("dram", "mg_in")])
        def xT_phase(l, raw):
            def evac(eng_, dst_, src_, j, v_):
                if raw:
                    cp(eng_, dst_, src_)
                elif eng_ == "act":
                    act(dst_, src_, AF.Identity, scale=scale_ap(j, l, v_), bias=shift_ap(j, l, v_))
                else:
                    ts("dve", dst_, src_, scale_ap(j, l, v_), shift_ap(j, l, v_), ALU.mult, ALU.add)
            for j in range(8):
                bA = bank()
                for i, t_ in enumerate((2, 3, 4, 5)):
                    tr(P(bA, i * 128, (i + 1) * 128), x_tm[:, t_, j * 128:(j + 1) * 128], ident[:, :])
                evac("act" if j % 2 == 0 else "dve", actT[:, j, 256:768], P(bA), j, 0)
            for j2 in range(0, 8, 2):
                bs = bank()
                for jj in range(2):
                    j = j2 + jj
                    for t_ in range(2):
                        tr(P(bs, jj * 256 + t_ * 128, jj * 256 + (t_ + 1) * 128), x_tm[:, t_, j * 128:(j + 1) * 128],
                           ident[:, :])
                for jj in range(2):
                    j = j2 + jj
                    evac("dve" if jj == 0 else "act", actT[:, j, 0:256], P(bs, jj * 256, jj * 256 + 256), j, 1)

        stage("a3")
        allgather_([[0, 1, 2, 3], [4, 5, 6, 7]], mg_in, mg_out, ("dram", "mg_in"), ("dram", "mg_out"))
        stage("a4")
        issue_load()
        dma("pool", Rm[:, :], rmat_d, "c_rm")
        dma("pool", BD[:, :], bd_d, "c_bd")
        issue_load()
        issue_load()
        dma("pool", maskT[:, :], mask_d, "c_mask")
        xT_phase(0, True)
        dma("sp", mo[0:96, :], mg_out.ap(), "mgo", reads=[("dram", "mg_out")])
        b3 = bank()
        tr(P(b3, 0, 96), mo[0:96, :], ident[0:96, 0:96])
        cp("dve", modfm[:, :], P(b3, 0, 96))
        ts("dve", onep[:, :], modfm[:, 32:64], 1.0, None, ALU.add)

        stage("adaln")

        def shift_ap(j, l, v):
            c = j * 4 + l * 2 + v
            return modfm[:, c:c + 1]

        def scale_ap(j, l, v):
            c = j * 4 + l * 2 + v
            return onep[:, c:c + 1]

        act(esink[:, :], esink[:, :], AF.Exp)
        stage("m1")
        for i in range(2):
            tt("dve", lamt[:, 2 * i, :], lamt[:, 2 * i, :], lamt[:, 2 * i + 1, :], ALU.mult)
            treduce_(small[:, 3 + i:4 + i], lamt[:, 2 * i, :])
        stage("m2")
        act(small[:, 3:5], small[:, 3:5], AF.Exp)
        stage("m3")
        tt("dve", small[:, 2:3], small[:, 4:5], small[:, 3:4], ALU.subtract)
        ts("dve", small[:, 2:3], small[:, 2:3], -LAM_INIT, None, ALU.add)
        ts("dve", subln_s[:, :], subln_s[:, :], 1.0 - LAM_INIT, None, ALU.mult)

        stage("misc")
        HALF = [(0, 512), (512, 768)]

        for l in range(2):
            even = (l == 0)
            NG = 4 if even else 8
            DV = 64 if even else 128
            VW = DV + 1
            NKT = 4 if even else 8
            li0 = 0 if even else 8

            def vview(reg, nt):
                return reg.m(lambda a: a[:, :, 0:NG * VW].rearrange("p t (g w) -> p t g w", w=VW))

            dma("sp", lng[:, :], ln_g_d[l].partition_broadcast(128), "c_lng")
            dma("sp", lnb[:, :], ln_b_d[l].partition_broadcast(128), "c_lnb")
            for v in range(2):
                dma("sp", gate_bc[v][:, :].m(lambda a: a.rearrange("p (j q) -> p j q", q=128)),
                    mg_out.ap()[64 + l * 2 + v:96:4, :].partition_broadcast(128), "c_gate%d" % v,
                    reads=[("dram", "mg_out")])
            stage("lc%d" % l)
            memset_("pool", vview(v_sb[:, :, :], 6).m(lambda a: a[:, :, :, DV:VW]), 1.0)
            memset_("pool", vview(vctx[:, :, :], 2).m(lambda a: a[:, :, :, DV:VW]), 1.0)

            stage("ms%d" % l)
            if l == 0:
                for j in range(8):
                    eA, eB = ("act", "dve") if j % 2 == 0 else ("dve", "act")
                    for (eng_, reg_, sa_, sh_) in ((eA, actT[:, j, 256:768], scale_ap(j, l, 0), shift_ap(j, l, 0)),
                                                   (eB, actT[:, j, 0:256], scale_ap(j, l, 1), shift_ap(j, l, 1))):
                        if eng_ == "act":
                            act(reg_, reg_, AF.Identity, scale=sa_, bias=sh_)
                        else:
                            ts("dve", reg_, reg_, sa_, sh_, ALU.mult, ALU.add)
            else:
                xT_phase(l, False)
            for t_ in range(6):
                ts("pool", x_tm[:, t_, :], x_tm[:, t_, :], ALPHA, 0.0, ALU.mult, ALU.add)
            stage("xT%d" % l)
            if even:
                for ti, (ck_, base) in enumerate(((cak, 0), (cbk, 2))):
                    for kv in range(2):
                        bb = bank()
                        for t_ in range(2):
                            tr(P(bb, t_ * 128, (t_ + 1) * 128),
                               ckl[:, t_, ti * 256 + kv * 128:ti * 256 + (kv + 1) * 128], ident[:, :])
                        cp("dve", kctx[:, base + kv, :], P(bb, 0, 256))
                for (cv_, g0) in ((cav, 0), (cbv, 2)):
                    for t_ in range(2):
                        dma("pool", vview(vctx[:, :, :], 2).m(lambda a, g0=g0, t_=t_: a[:, t_, g0:g0 + 2, 0:DV]),
                            cv_[t_ * 128:(t_ + 1) * 128, :].rearrange("p (g d) -> p g d", d=64), "vctx")
                for t_ in range(2):
                    dma("sp", ckl[:, t_, :], cck[t_ * 128:(t_ + 1) * 128, :], "ckl")
            else:
                for hh in range(8):
                    bb = bank()
                    for t_ in range(2):
                        tr(P(bb, t_ * 128, (t_ + 1) * 128), ckl[:, t_, hh * 128:(hh + 1) * 128], ident[:, :])
                    cp("dve" if hh % 2 else "act", kctx[:, hh, :], P(bb, 0, 256))
                for t_ in range(2):
                    dma("pool", vview(vctx[:, :, :], 2).m(lambda a, t_=t_: a[:, t_, :, 0:DV]),
                        ccv[t_ * 128:(t_ + 1) * 128, :].rearrange("p (g d) -> p g d", d=128), "vctx")

            stage("ctx%d" % l)
            cnt = {"fm": 0, "ko": 0, "vo": 0, "rope": 0}

            def rope(dst):
                i = cnt["rope"] % 2
                cnt["rope"] += 1
                bb = bankC()
                mm(P(bb, 0, 256), Rm[:, :], dst)
                tt("dve", rt1[i][:, :], P(bb, 0, 256), sinT[:, :], ALU.mult)
                tt("pool", rt2[i][:, :], dst, cosT[:, :], ALU.mult)
                tt("dve", dst, rt1[i][:, :], rt2[i][:, :], ALU.add)

            def sq_only(bks, i):
                act(sq[i][:, 0:512], P(bks[0]), AF.Square)
                act(sq[i][:, 512:768], P(bks[1], 0, 256), AF.Square)

            def rms_rest(i):
                c0, c1 = bankB(), bankB()
                mm(P(c0), BD[:, :], sq[i][:, 0:512])
                mm(P(c1, 0, 256), BD[:, :], sq[i][:, 512:768])
                act(rs[i][:, 0:512], P(c0), AF.Ln, scale=1.0 / 64, bias=small[:, 5:6])
                act(rs[i][:, 512:768], P(c1, 0, 256), AF.Ln, scale=1.0 / 64, bias=small[:, 5:6])
                act(rs[i][:, :], rs[i][:, :], AF.Exp, scale=-0.5)

            def fm_tile(li, ci, kind, dtile, norm, kout=None):
                w = W(li)
                i = cnt["fm"] % 3
                cnt["fm"] += 1
                bks = [bankA(), bankA()]
                for k in range(8):
                    for hb, (t0, t1) in enumerate(HALF):
                        mm(P(bks[hb], 0, t1 - t0), w[:, k, ci * 128:(ci + 1) * 128], actT[:, k, t0:t1],
                           start=(k == 0), stop=(k == 7))
                if kind == "q":
                    cp("act", qT[:, dtile, 0:512], P(bks[0]))
                    cp("dve", qT[:, dtile, 512:768], P(bks[1], 0, 256))
                    if norm:
                        sq_only(bks, i)

                    def postB_q():
                        if norm:
                            rms_rest(i)
                            stt_(qT[:, dtile, :], qT[:, dtile, :], small[:, 0:1], rs[i][:, :], ALU.mult, ALU.mult)

                    def postC_q():
                        rope(qT[:, dtile, 0:256])
                    return [postB_q, postC_q]
                cp("act", kf[i][:, 0:512], P(bks[0]))
                cp("dve", kf[i][:, 512:768], P(bks[1], 0, 256))
                if norm:
                    sq_only(bks, i)

                def postB_k():
                    if norm:
                        rms_rest(i)
                        stt_(kf[i][:, :], kf[i][:, :], small[:, 1:2], rs[i][:, :], ALU.mult, ALU.mult)

                def post_k():
                    cp("pool", kT[:, dtile, :], kf[i][:, :])
                    rope(kT[:, dtile, 0:256])
                    out_d, c0, wd = kout
                    bb = bankC()
                    for t_ in range(4):
                        tr(P(bb, t_ * 128, (t_ + 1) * 128), kf[i][:, 256 + t_ * 128:256 + (t_ + 1) * 128], ident[:, :])
                    oi = cnt["ko"] % 2
                    cnt["ko"] += 1
                    cp("dve", ko[oi][:, :, 0:wd], P(bb).m(lambda a: a.rearrange("p (t c) -> p t c", c=128)[:, :, 0:wd]))
                    dma("sp", out_d[:, c0:c0 + wd].rearrange("(t p) c -> p t c", p=128), ko[oi][:, :, 0:wd],
                        "ko%d" % oi, is_out=True)
                return [postB_k, post_k]

            def tm_v(li, ncol, gsel, outs):
                w = W(li)
                for t_ in range(6):
                    bb = bank()
                    for k in range(8):
                        mm(P(bb, 0, ncol), actT[:, k, t_ * 128:(t_ + 1) * 128], w[:, k, 0:ncol],
                           start=(k == 0), stop=(k == 7))
                    g0, ng = gsel
                    cp("act", vview(v_sb[:, t_:t_ + 1, :], 1).m(lambda a: a[:, 0, g0:g0 + ng, 0:DV]),
                       P(bb, 0, ncol).m(lambda a: a.rearrange("p (g d) -> p g d", d=DV)))
                    if t_ >= 2:
                        oi = cnt["vo"] % 3
                        cnt["vo"] += 1
                        cp("dve", vo[oi][:, 0:ncol], P(bb, 0, ncol))
                        for (od, c0, s0, n) in outs:
                            dma("sp", od[(t_ - 2) * 128:(t_ - 1) * 128, c0:c0 + n], vo[oi][:, s0:s0 + n],
                                "vo%d" % oi, is_out=True)

            def tm_g(li, gc0):
                w = W(li)
                for t_ in range(6):
                    bb = bank()
                    for k in range(8):
                        mm(P(bb), actT[:, k, t_ * 128:(t_ + 1) * 128], w[:, k, 0:512], start=(k == 0), stop=(k == 7))
                    act(g_sb[:, t_, gc0:gc0 + 512], P(bb), AF.Silu)
                    if not even:
                        tt("pool", g_sb[:, t_, gc0:gc0 + 512].m(lambda a: a.rearrange("p (h d) -> p h d", d=128)),
                           g_sb[:, t_, gc0:gc0 + 512].m(lambda a: a.rearrange("p (h d) -> p h d", d=128)),
                           subln_s[:, :].m(lambda a: a.unsqueeze(1).to_broadcast([128, 4, 128])), ALU.mult)

            def kick_gather():
                gi_, go_ = g_in[l], g_out[l]
                dma("sp", gi_.ap()[0:NKT * 128, :].rearrange("(t p) n -> p t n", p=128), kT[:, 0:NKT, 0:256], "gin",
                    writes=[("dram", "gin%d" % l)])
                fV = NG * DV // 256
                for t_ in range(2):
                    dma("sp", gi_.ap()[NKT * 128 + t_ * 128 * fV:NKT * 128 + (t_ + 1) * 128 * fV, :]
                        .rearrange("(p f) c -> p (f c)", p=128, f=fV)
                        .rearrange("p (g d) -> p g d", d=DV),
                        vview(v_sb[:, 0:2, :], 2).m(lambda a, t_=t_: a[:, t_, :, 0:DV]), "gin", writes=[("dram", "gin%d" % l)])
                allgather_([[0, 1, 2, 3], [4, 5, 6, 7]], gi_, go_, ("dram", "gin%d" % l), ("dram", "gout%d" % l))


            gview = g_out[l].ap().rearrange("(r x) n -> r x n", r=4)

            pend = []

            def run_fm(*a, **kw):
                p = fm_tile(*a, **kw)
                pend.append(p)
                if len(pend) >= 2 and pend[-2][0] is not None:
                    pend[-2][0]()
                    pend[-2][0] = None
                if len(pend) >= 3 and pend[-3][1] is not None:
                    pend[-3][1]()
                    pend[-3][1] = None

            def flush_fm():
                for p in pend:
                    for j_ in range(2):
                        if p[j_] is not None:
                            p[j_]()
                            p[j_] = None
                del pend[:]

            if even:
                for ci in range(4):
                    run_fm(0, ci, "q", ci, True)
                issue_load()
                for ci in range(4):
                    od = ak_o if ci < 2 else bk_o
                    run_fm(1, ci, "k", ci, ci < 2, kout=(od, (ci % 2) * 64, 64))
                issue_load()
                for ci in range(4):
                    run_fm(2, ci, "q", 4 + ci, False)
                issue_load()
                flush_fm()
                tm_v(3, 256, (0, 4), [(av_o, 0, 0, 128), (bv_o, 0, 128, 128)])
                issue_load()
                kick_gather()
                tm_g(4, 0)
                issue_load()
                tm_g(5, 512)
                issue_load()
            else:
                for h_ in range(2):
                    for ci in range(4):
                        run_fm(8 + h_, ci, "q", 4 * h_ + ci, False)
                    issue_load()
                for h_ in range(2):
                    for ci in range(4):
                        run_fm(10 + h_, ci, "k", 4 * h_ + ci, False, kout=(ck_o, (4 * h_ + ci) * 128, 128))
                    issue_load()
                flush_fm()
                for h_ in range(2):
                    tm_v(12 + h_, 512, (4 * h_, 4), [(cv_o, 512 * h_, 0, 512)])
                    issue_load()
                kick_gather()
                for h_ in range(2):
                    tm_g(14 + h_, 512 * h_)
                    issue_load()

            stage("inproj%d" % l)
            stage("gather%d" % l)
            cntA = {"pt": 0, "u": 0}

            def attention(c, maps, units, hook=None):
                sample = (c == 0)
                nkt = 10 if sample else 2
                oab = oa[c % 2]
                ustate = {}

                def load_unit(u):
                    sl = cntA["u"] % 3
                    cntA["u"] += 1
                    ktile, vg = units[u]
                    dma("sp", kl[sl][:, :].m(lambda a: a.rearrange("p (r n) -> p r n", r=4)),
                        gview[:, ktile * 128:(ktile + 1) * 128, :].rearrange("r p n -> p r n"), "kl%d" % sl,
                        reads=[("dram", "gout%d" % l)])
                    f = NG * DV // 256
                    vsrc = gview[:, NKT * 128:NKT * 128 + 256 * f, :] \
                        .rearrange("r (t p f) c -> p r t (f c)", p=128, f=f)[:, :, :, vg * DV:(vg + 1) * DV]
                    for r in range(4):
                        dma("sp", vl[sl][:, 2 * r:2 * r + 2, 0:DV], vsrc[:, r, :, :], "vl%d" % sl,
                            reads=[("dram", "gout%d" % l)])
                    memset_("pool", vl[sl][:, :, DV:VW], 1.0)
                    ustate[u] = sl

                def kk(mp, kt):
                    b_ = mp["base"]
                    if not sample:
                        return kT[b_:b_ + 64, mp["ktile"], c * 256 + kt * 128:c * 256 + (kt + 1) * 128]
                    if kt < 8:
                        return kl[ustate[mp["unit"]]][b_:b_ + 64, kt * 128:(kt + 1) * 128]
                    return kctx[b_:b_ + 64, mp["ktile"], (kt - 8) * 128:(kt - 7) * 128]

                def vv(mp, kt):
                    g = mp["vg"]
                    if not sample:
                        return v_sb[:, 2 * c + kt, g * VW:(g + 1) * VW]
                    if kt < 8:
                        return vl[ustate[mp["unit"]]][:, kt, 0:VW]
                    return vctx[:, kt - 8, g * VW:(g + 1) * VW]

                def qk(mp):
                    sl = cntA["pt"] % 3
                    cntA["pt"] += 1
                    mp["pt"] = sl
                    b_ = mp["base"]
                    qreg = qT[b_:b_ + 64, mp["qtile"], c * 256:(c + 1) * 256]
                    for kt0 in range(0, nkt, 2):
                        bb = bankS()
                        for d_ in range(2):
                            mm(P(bb, d_ * 256, (d_ + 1) * 256), kk(mp, kt0 + d_), qreg)
                        act(pT[sl][:, kt0:kt0 + 2, :].m(lambda a: a.rearrange("p t q -> p (t q)")), P(bb), AF.Exp,
                            scale=0.125)
                    if sample and mp["mask"]:
                        tt("dve", pT[sl][:, 0:8, :].m(lambda a: a.rearrange("p t q -> p (t q)")),
                           pT[sl][:, 0:8, :].m(lambda a: a.rearrange("p t q -> p (t q)")), maskT[:, :], ALU.mult)

                def pv(mp, bb, col0):
                    sl = mp["pt"]
                    for qt in range(2):
                        for kt in range(nkt):
                            mm(P(bb, col0 + qt * VW, col0 + (qt + 1) * VW), pT[sl][:, kt, qt * 128:(qt + 1) * 128],
                               vv(mp, kt), start=(kt == 0), stop=(kt == nkt - 1))

                def post_ab(grp, bb):
                    h0 = grp[0]["sinkh"]
                    pv4 = P(bb, 0, 4 * VW).m(lambda a: a.rearrange("p (k w) -> p k w", w=VW))
                    rsum = pv4.m(lambda a: a[:, :, DV:VW].rearrange("p k o -> p (k o)"))
                    if grp[0]["sink"]:
                        tt("dve", rr[:, 0:4].m(lambda a: a.rearrange("p (g q) -> p g q", q=2)),
                           rsum.m(lambda a: a.rearrange("p (g q) -> p g q", q=2)),
                           esink[:, h0:h0 + 2].m(lambda a: a.unsqueeze(2).to_broadcast([128, 2, 2])), ALU.add)
                        recip(rr[:, 0:4], rr[:, 0:4])
                    else:
                        recip(rr[:, 0:4], rsum)
                    oc = grp[0]["ocol"]
                    for gi in range(2):
                        tt("dve", oab[:, :, oc + gi * 64:oc + (gi + 1) * 64],
                           P(bb, gi * 2 * VW, (gi + 1) * 2 * VW).m(lambda a: a.rearrange("p (q w) -> p q w", w=VW)[:, :, 0:DV]),
                           rr[:, 2 * gi:2 * gi + 2].m(lambda a: a.unsqueeze(2).to_broadcast([128, 2, DV])), ALU.mult)

                def post_c(grp, bX, bY):
                    hh = grp[0]["sinkh"]
                    X = P(bX, 0, 2 * VW).m(lambda a: a.rearrange("p (q w) -> p q w", w=VW))
                    Y = P(bY, 0, 2 * VW).m(lambda a: a.rearrange("p (q w) -> p q w", w=VW))
                    recip(rr[:, 0:2], X.m(lambda a: a[:, :, DV:VW].rearrange("p q o -> p (q o)")))
                    recip(rr[:, 2:4], Y.m(lambda a: a[:, :, DV:VW].rearrange("p q o -> p (q o)")))
                    ts("dve", rr[:, 4:6], rr[:, 2:4], small[:, 2:3], None, ALU.mult)
                    tt("dve", o1[:, :, :], X.m(lambda a: a[:, :, 0:DV]),
                       rr[:, 0:2].m(lambda a: a.unsqueeze(2).to_broadcast([128, 2, DV])), ALU.mult)
                    for qt in range(2):
                        stt_(oab[:, qt, hh * 128:(hh + 1) * 128], Y.m(lambda a, qt=qt: a[:, qt, 0:DV]),
                             rr[:, 4 + qt:5 + qt], o1[:, qt, :], ALU.mult, ALU.add)

                pend = None
                groups = [maps[i:i + 2] for i in range(0, len(maps), 2)]
                gstate = []
                loaded = set()
                nu = len(units)
                if sample:
                    load_unit(0)
                    loaded.add(0)
                flat = []
                for g in groups:
                    for mp in g:
                        flat.append((mp, g))
                pvbank = {}

                def do_pv(mp, g):
                    gid = id(g)
                    if even:
                        if gid not in pvbank:
                            pvbank[gid] = bankPV()
                        gi = 0 if mp is g[0] else 1
                        pv(mp, pvbank[gid], gi * 2 * VW)
                        if mp is g[-1]:
                            post_ab(g, pvbank[gid])
                    else:
                        if gid not in pvbank:
                            pvbank[gid] = [bankPV(), bankPV()]
                        gi = 0 if mp is g[0] else 1
                        pv(mp, pvbank[gid][gi], 0)
                        if mp is g[-1]:
                            post_c(g, pvbank[gid][0], pvbank[gid][1])

                for idx, (mp, g) in enumerate(flat):
                    if idx == 2 and hook is not None:
                        hook()
                    if sample:
                        u = mp["unit"]
                        if u + 1 < nu and (u + 1) not in loaded:
                            load_unit(u + 1)
                            loaded.add(u + 1)
                    qk(mp)
                    if pend is not None:
                        do_pv(*pend)
                    pend = (mp, g)
                do_pv(*pend)

                if not even:
                    slq = cntA["pt"] % 3
                    cntA["pt"] += 1
                    sqb = pT[slq][:, 0:8, :]
                    act(sqb.m(lambda a: a.rearrange("p t q -> p (t q)")),
                        oab[:, :, :].m(lambda a: a.rearrange("p q d -> p (q d)")), AF.Square)
                    treduce_(ssb[:, 0:16], sqb.m(lambda a: a.rearrange("p t q -> p (t q)").rearrange("p (g d) -> p g d", d=128)))
                    act(ssb[:, 0:16], ssb[:, 0:16], AF.Ln, scale=1.0 / 128, bias=small[:, 5:6])
                    act(ssb[:, 0:16], ssb[:, 0:16], AF.Exp, scale=-0.5)
                    tt("dve", oab[:, :, :].m(lambda a: a.rearrange("p q (h d) -> p (q h) d", d=128)),
                       oab[:, :, :].m(lambda a: a.rearrange("p q (h d) -> p (q h) d", d=128)),
                       ssb[:, 0:16].m(lambda a: a.unsqueeze(2).to_broadcast([128, 16, 128])), ALU.mult)
                tt("pool", oab[:, 0, :], oab[:, 0, :], g_sb[:, 2 * c, :], ALU.mult)
                tt("dve", oab[:, 1, :], oab[:, 1, :], g_sb[:, 2 * c + 1, :], ALU.mult)

                def epi_pe(bank_fn=bank):
                    for qt in range(2):
                        t_ = 2 * c + qt
                        for hf in range(2):
                            bb = bank_fn()
                            for j in range(4):
                                tr(P(bb, j * 128, (j + 1) * 128), oab[:, qt, (4 * hf + j) * 128:(4 * hf + j + 1) * 128],
                                   ident[:, :])
                            cp("act" if hf == 0 else "dve", actT[:, 4 * hf:4 * hf + 4, t_ * 128:(t_ + 1) * 128],
                               P(bb).m(lambda a: a.rearrange("p (j q) -> p j q", q=128)))
                return epi_pe

            def make_maps():
                maps, units = [], []
                if even:
                    for typ in range(2):
                        for kv in range(2):
                            units.append((typ * 2 + kv, typ * 2 + kv))
                            for hq in range(4):
                                h = kv * 4 + hq
                                maps.append(dict(ktile=typ * 2 + kv, qtile=typ * 4 + h // 2, base=(h % 2) * 64,
                                                 vg=typ * 2 + kv, unit=typ * 2 + kv, sinkh=h, sink=(typ == 1),
                                                 mask=(typ == 1), ocol=typ * 512 + (h - h % 2) * 64))
                else:
                    for hh in range(8):
                        units.append((hh, hh))
                        for m_ in range(2):
                            maps.append(dict(ktile=hh, qtile=hh, base=m_ * 64, vg=hh, unit=hh, sinkh=hh, sink=False,
                                             mask=False, ocol=hh * 128))
                return maps, units

            wl = [6, 7] if even else [16, 17]
            yic = {"n": 0}

            def outproj_ln(tiles, bank_fn=bank, split=False):
              for t_ in tiles:
                  v = 1 if t_ < 2 else 0
                  yi = yic["n"]
                  y_ = yt[yi % 2]
                  stt = stts[yi % 2]
                  yic["n"] += 1
                  bks = [bank_fn(), bank_fn()]
                  for hf in range(2):
                      for k in range(8):
                          mm(P(bks[hf]), actT[:, k, t_ * 128:(t_ + 1) * 128], W(wl[hf])[:, k, 0:512],
                             start=(k == 0), stop=(k == 7))
                  for hf in range(2):
                      tt("dve", y_[:, hf * 512:(hf + 1) * 512], P(bks[hf]), gate_bc[v][:, hf * 512:(hf + 1) * 512], ALU.mult)
                  if split:
                      tt("pool", y_[:, 0:512], y_[:, 0:512], x_tm[:, t_, 0:512], ALU.add)
                      tt("dve", y_[:, 512:1024], y_[:, 512:1024], x_tm[:, t_, 512:1024], ALU.add)
                  else:
                      tt("pool", y_[:, :], y_[:, :], x_tm[:, t_, :], ALU.add)
                  for hf in range(2):
                      bnstats_(stt[:, 32 + hf * 6:38 + hf * 6], y_[:, hf * 512:(hf + 1) * 512])
                  bnaggr_(stt[:, 0:2], stt[:, 32:44])
                  act(stt[:, 2:3], stt[:, 1:2], AF.Ln, bias=small[:, 5:6])
                  act(stt[:, 2:3], stt[:, 2:3], AF.Exp, scale=-0.5)
                  ts("dve", stt[:, 3:4], stt[:, 0:1], -1.0, stt[:, 2:3], ALU.mult, ALU.mult)
                  act(y_[:, :], y_[:, :], AF.Identity, scale=stt[:, 2:3], bias=stt[:, 3:4])
                  tt("dve", y_[:, :], y_[:, :], lng[:, :], ALU.mult)
                  if split:
                      tt("pool", x_tm[:, t_, 0:512], y_[:, 0:512], lnb[:, 0:512], ALU.add)
                      tt("dve", x_tm[:, t_, 512:1024], y_[:, 512:1024], lnb[:, 512:1024], ALU.add)
                  else:
                      tt("pool", x_tm[:, t_, :], y_[:, :], lnb[:, :], ALU.add)
                  if l == 1:
                      dma("pool", y_o[t_ * 128:(t_ + 1) * 128, :], x_tm[:, t_, :], "yo%d" % (t_ % 2), is_out=True)
                  elif DBG:
                      dma("sp", dbg_o[t_ * 128:(t_ + 1) * 128, :], x_tm[:, t_, :], "yo%d" % (t_ % 2), is_out=True)

            maps, units = make_maps()
            epi1 = attention(1, maps, units)
            maps, units = make_maps()
            epi2 = attention(2, maps, units)
            epi1()
            outproj_ln((2, 3), split=True)

            def mid_hook():
                epi2(bankS)
                outproj_ln((4, 5), bankS)
            maps, units = make_maps()
            epi0 = attention(0, maps, units, hook=mid_hook)
            stage("attn%d_0" % l)
            epi0()
            outproj_ln((0, 1), split=True)
            if even:
                for _ in range(2):
                    issue_load()
            else:
                pass


    try:
        body()
    except _Stop:
        pass

    fin = S.op("sp", None, reads=[], writes=[])
    for o in S.out_ops:
        fin.dwaits[o.key] = 16 * S.dcnt[o.key]
    S.finalize()

    sem_names = {}
    import contextlib
    with contextlib.ExitStack() as es:
        csem = {e: es.enter_context(nc.semaphore("c_" + e)) for e in ("pe", "act", "dve", "pool")}
        dsem = {k: es.enter_context(nc.semaphore("d_" + k)) for k in S.dcnt}
        ccsem = [es.enter_context(nc.semaphore("cc%d" % i)) for i in range(S.ncc)]
        block = es.enter_context(nc.Block())

        def run(engname, e):
            known = {}

            def wait(sem, val, tag):
                if known.get(tag, 0) < val:
                    e.wait_ge(sem, val)
                    known[tag] = val

            for o in S.q[engname]:
                need = {}
                for d_ in o.deps:
                    if d_.kind == "c":
                        tag = ("c", d_.eng)
                        need[tag] = max(need.get(tag, 0), d_.sig)
                    elif d_.kind == "cc":
                        need[("cc", d_.ccid)] = 1
                for k_, v_ in o.dwaits.items():
                    need[("d", k_)] = max(need.get(("d", k_), 0), v_)
                for tag, val in need.items():
                    if tag[0] == "c":
                        wait(csem[tag[1]], val, tag)
                    elif tag[0] == "d":
                        wait(dsem[tag[1]], val, tag)
                    else:
                        wait(ccsem[tag[1]], val, tag)
                if o.fn is None:
                    continue
                ins = o.fn(e)
                if o.kind == "c":
                    if o.need_inc:
                        ins.then_inc(csem[engname], 1)
                elif o.kind == "d":
                    ins.then_inc(dsem[o.key], 16)
                else:
                    ins.then_inc(ccsem[o.ccid])

        @block.tensor
        def _(e):
            run("pe", e)

        @block.scalar
        def _(e):
            run("act", e)

        @block.vector
        def _(e):
            run("dve", e)

        @block.gpsimd
        def _(e):
            run("pool", e)

        @block.sync
        def _(e):
            run("sp", e)
    return nc


def _consts(r):
    ident = np.eye(128, dtype=np.float32)
    rmat = np.zeros((128, 128), np.float32)
    for m in range(128):
        if (m % 32) < 16:
            rmat[m + 16, m] = -1.0
        else:
            rmat[m - 16, m] = 1.0
    bd = np.zeros((128, 128), np.float32)
    bd[0:64, 0:64] = 1.0
    bd[64:128, 64:128] = 1.0
    t = 256 * r + np.arange(256)
    row = (t // 64).astype(np.float64)
    col = (t % 64).astype(np.float64)
    freqs = (10000.0 ** (-(np.arange(16, dtype=np.float32) / np.float32(16)))).astype(np.float64)
    ang = np.zeros((128, 256))
    for p in range(128):
        d = p % 64
        f = freqs[d % 16]
        ang[p] = (row if d < 32 else col) * f
    cosT = np.cos(ang).astype(np.float32)
    sinT = np.sin(ang).astype(np.float32)
    j = (np.arange(8)[None, :, None] * 128 + np.arange(128)[:, None, None])
    i = (256 * r + np.arange(256))[None, None, :]
    mask = (np.abs(j - i) <= 128).astype(np.float32).reshape(128, 2048)
    return dict(ident=ident, rmat=rmat, bdmat=bd, cosT=cosT, sinT=sinT, maskd=mask)


_NC_CACHE = {}


def make_in_maps(x_prompt, x_sample, cache_a_k, cache_a_v, cache_b_k, cache_b_v, cache_c_k, cache_c_v,
                 c, c_ctx, w_mod, b_mod, ln_g, ln_b, w_in_even, w_out_even, q_norm_a, k_norm_a, sink_b,
                 w_in_odd, w_out_odd, lambda_q1, lambda_k1, lambda_q2, lambda_k2, subln_c):
    f = lambda a: np.ascontiguousarray(np.asarray(a, dtype=np.float32))
    x_prompt, x_sample = f(x_prompt), f(x_sample)
    in_maps = []
    for i in range(8):
        b, r = i // 4, i % 4
        m = dict(
            xs=np.concatenate([x_sample[b, 256 * r:256 * (r + 1)], x_prompt[2 * i], x_prompt[2 * i + 1]], 0),
            cak=f(cache_a_k)[b, 0].reshape(256, 128), cav=f(cache_a_v)[b, 0].reshape(256, 128),
            cbk=f(cache_b_k)[b, 0].reshape(256, 128), cbv=f(cache_b_v)[b, 0].reshape(256, 128),
            cck=f(cache_c_k)[b, 0].reshape(256, 1024), ccv=f(cache_c_v)[b, 0].reshape(256, 1024),
            cvec=np.stack([f(c_ctx), f(c)[b]], 0),
            wmod_sh=f(f(w_mod)[:, :, 768 * r:768 * (r + 1)]), bmod_sh=f(f(b_mod)[:, 768 * r:768 * (r + 1)]),
            ln_g=f(ln_g), ln_b=f(ln_b),
            w_in_e=f(w_in_even)[0], w_out_e=f(w_out_even)[0], w_in_o=f(w_in_odd)[0], w_out_o=f(w_out_odd)[0],
            qn=f(q_norm_a)[0], kn=f(k_norm_a)[0], sink=f(sink_b)[0],
            lq1=f(lambda_q1)[0], lk1=f(lambda_k1)[0], lq2=f(lambda_q2)[0], lk2=f(lambda_k2)[0],
            subln=f(subln_c)[0],
        )
        m.update(_consts(r))
        in_maps.append({k: np.ascontiguousarray(v) for k, v in m.items()})
    return in_maps


def assemble(R):
    y_prompt = np.zeros((16, 256, 1024), np.float32)
    y_sample = np.zeros((2, 1024, 1024), np.float32)
    nak = np.zeros((16, 1, 256, 2, 64), np.float32)
    nav, nbk, nbv = np.zeros_like(nak), np.zeros_like(nak), np.zeros_like(nak)
    nck = np.zeros((16, 1, 256, 8, 128), np.float32)
    ncv = np.zeros_like(nck)
    for i in range(8):
        b, r = i // 4, i % 4
        y = R[i]["y"]
        y_sample[b, 256 * r:256 * (r + 1)] = y[0:256]
        for s in range(2):
            y_prompt[2 * i + s] = y[256 * (s + 1):256 * (s + 2)]
            sl = slice(256 * s, 256 * (s + 1))
            nak[2 * i + s, 0] = R[i]["ak"][sl].reshape(256, 2, 64)
            nav[2 * i + s, 0] = R[i]["av"][sl].reshape(256, 2, 64)
            nbk[2 * i + s, 0] = R[i]["bk"][sl].reshape(256, 2, 64)
            nbv[2 * i + s, 0] = R[i]["bv"][sl].reshape(256, 2, 64)
            nck[2 * i + s, 0] = R[i]["ck"][sl].reshape(256, 8, 128)
            ncv[2 * i + s, 0] = R[i]["cv"][sl].reshape(256, 8, 128)
    return (y_prompt, y_sample, nak, nav, nbk, nbv, nck, ncv)


def kernel(**inputs):
    in_maps = make_in_maps(**inputs)
    if "nc" not in _NC_CACHE:
        _NC_CACHE["nc"] = build_nc()
    res = run_bass_kernel_spmd(_NC_CACHE["nc"], in_maps, core_ids=list(range(8)))
    return assemble(res.results)
```

```python
import math
import numpy as np
import concourse.bass as bass
import concourse.mybir as mybir
from concourse.bass_utils import run_bass_kernel_spmd

F32 = mybir.dt.float32
BF16 = mybir.dt.bfloat16
AF = mybir.ActivationFunctionType
ALU = mybir.AluOpType
AX = mybir.AxisListType

D = 1024
EPS = 1e-6
ALPHA = (2 * 2) ** 0.25
LAM_INIT = 0.8 - 0.6 * math.exp(-0.3 * 1)
BLK = 256
ENGS = ["pe", "act", "dve", "pool", "sp"]


class Reg:
    def __init__(s, ap, toks):
        s.ap, s.toks = ap, toks

    def m(s, f):
        return Reg(f(s.ap), s.toks)


class T:
    def __init__(s, nc, name, shape, dt, arena=None, off=0, tok=None):
        s.shape = list(shape)
        s.dt = dt
        s.isz = 4 if dt == F32 else 2
        n = int(np.prod(shape))
        if arena is None:
            s.h = nc.alloc_sbuf_tensor("sb_" + name, [128, n], dt)
            base = s.h[:, :]
            s.tok = name
            s.boff = 0
        else:
            a = arena[:, off // 2:(off + n * s.isz) // 2]
            base = a.bitcast(F32) if dt == F32 else a
            s.tok = tok
            s.boff = off
        if len(shape) == 1:
            s.full = base
        elif len(shape) == 2:
            s.full = base.rearrange("p (a b) -> p a b", b=shape[1])
        else:
            s.full = base.rearrange("p (a b c) -> p a b c", b=shape[1], c=shape[2])
        st = [1]
        for d in reversed(shape[1:]):
            st.insert(0, st[0] * d)
        s.strides = st

    def __getitem__(s, idx):
        if not isinstance(idx, tuple):
            idx = (idx,)
        ap = s.full[idx]
        lo = hi = 0
        for d, (n, st) in enumerate(zip(s.shape, s.strides)):
            i = idx[d + 1] if d + 1 < len(idx) else slice(None)
            if isinstance(i, int):
                a, b = i, i + 1
            else:
                a = i.start or 0
                b = n if i.stop is None else i.stop
            lo += a * st
            hi += (b - 1) * st
        lo_b = s.boff + lo * s.isz
        hi_b = s.boff + (hi + 1) * s.isz
        return Reg(ap, [(s.tok, k) for k in range(lo_b // BLK, (hi_b - 1) // BLK + 1)])


class Op:
    __slots__ = ("eng", "fn", "kind", "key", "deps", "dwaits", "need_inc", "sig", "ccid")

    def __init__(s, eng, fn, kind, key):
        s.eng, s.fn, s.kind, s.key = eng, fn, kind, key
        s.deps = []
        s.dwaits = {}
        s.need_inc = False
        s.sig = 0
        s.ccid = None


class Sched:
    def __init__(s):
        s.q = {e: [] for e in ENGS}
        s.W = {}
        s.R = {}
        s.dcnt = {}
        s.ncc = 0
        s.out_ops = []

    @staticmethod
    def _toks(lst):
        out = []
        for r in lst:
            if r is None:
                continue
            if isinstance(r, Reg):
                out.extend(r.toks)
            elif isinstance(r, (tuple, str)):
                out.append(r)
            else:
                raise TypeError(type(r))
        return out

    def _dep(s, o, w):
        if w is o:
            return
        if w.kind == "d":
            if o.kind == "d" and o.key == w.key:
                return
            o.dwaits[w.key] = 16 * s.dcnt[w.key]
            t = ("sem", w.key)
            s.R.setdefault(t, []).append(o)
        else:
            if w.eng == "pe" and o.eng == "pe" and o.kind == "c":
                return
            o.deps.append(w)
            w.need_inc = True

    def op(s, eng, fn, reads=(), writes=(), kind="c", key=None):
        o = Op(eng, fn, kind, key)
        rt = s._toks(reads)
        wt = s._toks(writes)
        if eng != "pe":
            wt = wt + [t for t in rt if isinstance(t, tuple) and t[0] == "ps" and t not in wt]
        seen = set()
        for t in rt:
            for w in s.W.get(t, ()):
                if id(w) not in seen:
                    seen.add(id(w))
                    s._dep(o, w)
        for t in wt:
            for w in s.W.get(t, ()):
                if id(w) not in seen:
                    seen.add(id(w))
                    s._dep(o, w)
            for w in s.R.get(t, ()):
                if id(w) not in seen:
                    seen.add(id(w))
                    s._dep(o, w)
        if kind == "d":
            t = ("sem", key)
            for w in s.R.get(t, ()):
                if id(w) not in seen:
                    seen.add(id(w))
                    s._dep(o, w)
            s.dcnt[key] = s.dcnt.get(key, 0) + 1
            s.R[t] = []
        if kind == "cc":
            o.ccid = s.ncc
            s.ncc += 1
        for t in rt:
            s.R.setdefault(t, []).append(o)
        for t in wt:
            s.W[t] = [o]
            s.R[t] = []
        s.q[eng].append(o)
        return o

    def finalize(s):
        for e in ENGS:
            c = 0
            for o in s.q[e]:
                if o.kind == "c" and o.need_inc:
                    c += 1
                    o.sig = c


class _Stop(Exception):
    pass


def build_nc(stop=None):
    nc = bass.Bass("TRN2", target_bir_lowering=False)
    S = Sched()

    def stage(name):
        if stop == name:
            raise _Stop()

    def din(name, shape, dt=F32):
        return nc.dram_tensor(name, shape, dt, kind="ExternalInput").ap()

    def dout(name, shape):
        return nc.dram_tensor(name, shape, F32, kind="ExternalOutput").ap()

    xs = din("xs", [768, D])
    cak, cav, cbk, cbv = (din(n, [256, 128]) for n in ("cak", "cav", "cbk", "cbv"))
    cck, ccv = din("cck", [256, D]), din("ccv", [256, D])
    cvec = din("cvec", [2, D])
    wmod_sh = din("wmod_sh", [2, D, 768])
    bmod_sh = din("bmod_sh", [2, 768])
    ln_g_d, ln_b_d = din("ln_g", [2, D]), din("ln_b", [2, D])
    w_in_e, w_out_e = din("w_in_e", [D, 2560]), din("w_out_e", [D, D])
    w_in_o, w_out_o = din("w_in_o", [D, 4096]), din("w_out_o", [D, D])
    qn_d, kn_d = din("qn", [64]), din("kn", [64])
    sink_d = din("sink", [8])
    lq1_d, lk1_d, lq2_d, lk2_d = (din(n, [64]) for n in ("lq1", "lk1", "lq2", "lk2"))
    subln_d = din("subln", [128])
    ident_d = din("ident", [128, 128])
    rmat_d, bd_d = din("rmat", [128, 128]), din("bdmat", [128, 128])
    cos_d, sin_d = din("cosT", [128, 256]), din("sinT", [128, 256])
    mask_d = din("maskd", [128, 2048])

    y_o = dout("y", [768, D])
    ak_o, av_o, bk_o, bv_o = (dout(n, [512, 128]) for n in ("ak", "av", "bk", "bv"))
    ck_o, cv_o = dout("ck", [512, D]), dout("cv", [512, D])
    import os
    DBG = bool(os.environ.get("KDBG"))
    dbg_o = dout("dbg", [768, D]) if DBG else None

    mg_in = nc.dram_tensor("mg_in", [24, 128], F32)
    mg_out = nc.dram_tensor("mg_out", [96, 128], F32)
    g_in = [nc.dram_tensor("g0_in", [768, 256], BF16), nc.dram_tensor("g1_in", [2048, 256], BF16)]
    g_out = [nc.dram_tensor("g0_out", [4 * 768, 256], BF16), nc.dram_tensor("g1_out", [4 * 2048, 256], BF16)]
    GROWS = [768, 2048]

    x_tm = T(nc, "x_tm", [6, D], F32)
    actT = T(nc, "actT", [8, 768], BF16)
    qT = T(nc, "qT", [8, 768], BF16)
    kT = T(nc, "kT", [8, 768], BF16)
    v_sb = T(nc, "v_sb", [6, 1032], BF16)
    g_sb = T(nc, "g_sb", [6, D], BF16)
    wring = [T(nc, "wr%d" % i, [8, 512], BF16) for i in range(3)]
    ident = T(nc, "ident", [128], F32)
    Rm = T(nc, "Rm", [128], BF16)
    BD = T(nc, "BD", [128], BF16)
    cosT = T(nc, "cosT", [256], F32)
    sinT = T(nc, "sinT", [256], F32)
    maskT = T(nc, "maskT", [2048], BF16)
    modfm = T(nc, "modfm", [96], F32)
    onep = T(nc, "onep", [32], F32)
    small = T(nc, "small", [64], F32)
    esink = T(nc, "esink", [8], F32)
    subln_s = T(nc, "subln_s", [128], F32)
    lamt = T(nc, "lamt", [4, 64], F32)
    kctx = T(nc, "kctx", [8, 256], BF16)
    vctx = T(nc, "vctx", [2, 1032], BF16)
    gate_bc = [T(nc, "gate_bc%d" % v, [D], F32) for v in range(2)]
    lng = T(nc, "lng", [D], F32)
    lnb = T(nc, "lnb", [D], F32)
    ckl = T(nc, "ckl", [2, D], F32)
    stk = T(nc, "stk", [128], F32)
    scT = T(nc, "scT", [16], BF16)
    bmT = T(nc, "bmT", [12], F32)
    resm = T(nc, "resm", [24], F32)
    resT = T(nc, "resT", [128], F32)
    mo = T(nc, "mo", [128], F32)
    wm = [T(nc, "wm0", [8, 768], BF16, arena=qT.h, off=0, tok="qT"),
          T(nc, "wm1", [8, 768], BF16, arena=kT.h, off=0, tok="kT")]
    ARN = 46 * 1024
    arena = nc.alloc_sbuf_tensor("arena", [128, ARN // 2], BF16)

    def AT(name, shape, dt, off):
        t = T(nc, name, shape, dt, arena=arena, off=off, tok="ARENA")
        assert off + int(np.prod(shape)) * t.isz <= ARN, name
        return t

    K = 1024
    kf = [AT("kf%d" % i, [768], F32, i * 3 * K) for i in range(3)]
    ko = [AT("ko%d" % i, [4, 128], F32, 9 * K + i * 2 * K) for i in range(2)]
    vo = [AT("vo%d" % i, [512], F32, 13 * K + i * 2 * K) for i in range(3)]
    sq = [AT("sq%d" % i, [768], BF16, 19 * K + i * 1536) for i in range(3)]
    rs = [AT("rs%d" % i, [768], F32, 24 * K + i * 3 * K) for i in range(3)]
    rt1 = [AT("rt1%d" % i, [256], F32, 33 * K + i * K) for i in range(2)]
    rt2 = [AT("rt2%d" % i, [256], F32, 35 * K + i * K) for i in range(2)]
    kl = [AT("kl%d" % i, [1024], BF16, i * 2 * K) for i in range(3)]
    vl = [AT("vl%d" % i, [8, 129], BF16, 6 * K + i * 2112) for i in range(3)]
    pT = [AT("pT%d" % i, [10, 256], BF16, 13 * K + i * 5 * K) for i in range(3)]
    oa = [AT("oa%d" % i, [2, D], F32, 28 * K + i * 8 * K) for i in range(2)]
    rr = AT("rr", [64], F32, 44 * K)
    o1 = AT("o1", [2, 128], F32, 44 * K + 512)
    ssb = AT("ssb", [64], F32, 45 * K + 512)
    yt = [T(nc, "yt%d" % i, [D], F32) for i in range(2)]
    stts = [T(nc, "stt%d" % i, [64], F32) for i in range(2)]

    ps = nc.alloc_psum_tensor("ps", [128, 8, 512], F32)

    def P(b, lo=0, hi=512, p0=0, p1=128):
        return Reg(ps[p0:p1, b, lo:hi], [("ps", b)])

    rrb = {"n": 0, "s": 0, "pv": 0}

    def bank():
        b = rrb["n"] % 8
        rrb["n"] += 1
        return b

    def bankS():
        b = rrb["s"] % 4
        rrb["s"] += 1
        return b

    def bankA():
        b = rrb.get("a", 0) % 4
        rrb["a"] = rrb.get("a", 0) + 1
        return b

    def bankB():
        b = 4 + rrb.get("b", 0) % 2
        rrb["b"] = rrb.get("b", 0) + 1
        return b

    def bankC():
        b = 6 + rrb.get("c", 0) % 2
        rrb["c"] = rrb.get("c", 0) + 1
        return b

    def bankPV():
        b = 4 + rrb["pv"] % 4
        rrb["pv"] += 1
        return b

    def mm(out, lhsT, rhs, start=True, stop=True):
        S.op("pe", lambda e: e.matmul(out.ap, lhsT.ap, rhs.ap, start=start, stop=stop),
             reads=[lhsT, rhs], writes=[out])

    def tr(out, in_, idr):
        S.op("pe", lambda e: e.transpose(out.ap, in_.ap, idr.ap), reads=[in_, idr], writes=[out])

    def act(out, in_, func, scale=1.0, bias=0.0, eng="act"):
        rd = [in_]
        sc = scale.ap if isinstance(scale, Reg) else scale
        bi = bias.ap if isinstance(bias, Reg) else bias
        if isinstance(scale, Reg):
            rd.append(scale)
        if isinstance(bias, Reg):
            rd.append(bias)
        S.op("act", lambda e: e.activation(out.ap, in_.ap, func, bias=bi, scale=sc), reads=rd, writes=[out])

    def tt(eng, out, in0, in1, op):
        S.op(eng, lambda e: e.tensor_tensor(out.ap, in0.ap, in1.ap, op), reads=[in0, in1], writes=[out])

    def ts(eng, out, in0, s1, s2, op0, op1=None):
        rd = [in0]
        a1 = s1.ap if isinstance(s1, Reg) else s1
        a2 = s2.ap if isinstance(s2, Reg) else s2
        if isinstance(s1, Reg):
            rd.append(s1)
        if isinstance(s2, Reg):
            rd.append(s2)
        if op1 is None:
            S.op(eng, lambda e: e.tensor_scalar(out.ap, in0.ap, a1, None, op0), reads=rd, writes=[out])
        else:
            S.op(eng, lambda e: e.tensor_scalar(out.ap, in0.ap, a1, a2, op0, op1), reads=rd, writes=[out])

    def stt_(out, in0, sc, in1, op0, op1):
        rd = [in0, in1]
        a = sc.ap if isinstance(sc, Reg) else sc
        if isinstance(sc, Reg):
            rd.append(sc)
        S.op("dve", lambda e: e.scalar_tensor_tensor(out.ap, in0.ap, a, in1.ap, op0, op1), reads=rd, writes=[out])

    def cp(eng, out, in_):
        if eng == "act":
            S.op("act", lambda e: e.copy(out.ap, in_.ap), reads=[in_], writes=[out])
        else:
            S.op(eng, lambda e: e.tensor_copy(out.ap, in_.ap), reads=[in_], writes=[out])

    def dma(q, out, in_, key, reads=(), writes=(), is_out=False):
        oa_ = out.ap if isinstance(out, Reg) else out
        ia_ = in_.ap if isinstance(in_, Reg) else in_
        rd = list(reads) + ([in_] if isinstance(in_, Reg) else [])
        wr = list(writes) + ([out] if isinstance(out, Reg) else [])
        o = S.op(q, lambda e: e.dma_start(out=oa_, in_=ia_), reads=rd, writes=wr, kind="d", key=key)
        if is_out:
            S.out_ops.append(o)
        return o

    def recip(out, in_):
        S.op("dve", lambda e: e.reciprocal(out.ap, in_.ap), reads=[in_], writes=[out])

    def memset_(eng, reg, val):
        S.op(eng, lambda e: e.memset(reg.ap, val), writes=[reg])

    def bnstats_(out, in_):
        S.op("dve", lambda e: e.bn_stats(out.ap, in_.ap), reads=[in_], writes=[out])

    def bnaggr_(out, in_):
        S.op("dve", lambda e: e.bn_aggr(out.ap, in_.ap), reads=[in_], writes=[out])

    def treduce_(out, in_):
        S.op("dve", lambda e: e.tensor_reduce(out=out.ap, in_=in_.ap, axis=AX.X, op=ALU.add), reads=[in_], writes=[out])

    def allgather_(groups, din_, dout_, rtok, wtok):
        S.op("pool", lambda e: e.collective_compute("AllGather", ALU.bypass, replica_groups=groups,
                                                    ins=[din_.ap().opt()], outs=[dout_.ap().opt()]),
             reads=[rtok], writes=[wtok], kind="cc")

    memset_("dve", small[:, 5:6], EPS)

    def body():
        dma("sp", ident[:, :], ident_d, "c_id")
        stk_toks = [("stkp", n) for n in range(15)]
        S.op("pool", lambda e: e.memset(stk.full[:, :], 0.0), writes=stk_toks)
        n_ = 0
        for v in range(2):
            dma("sp", Reg(stk.full[8 * v:8 * v + 8, :], [stk_toks[n_]]), cvec[v].rearrange("(k p) -> k p", p=128), "c_stk%d" % n_)
            n_ += 1
        for t in range(6):
            for l in range(2):
                r = 16 + t * 2 + l
                dma("sp", Reg(stk.full[r:r + 1, :], [stk_toks[n_]]), bmod_sh[l:l + 1, t * 128:(t + 1) * 128], "c_stk%d" % n_)
                n_ += 1
        stk_all = Reg(stk.full[0:28, :], stk_toks)
        for l in range(2):
            dma("pool", wm[l][:, :, :], wmod_sh[l].rearrange("(k p) n -> p k n", p=128), "c_wm%d" % l)
        for tt_i in range(6):
            dma("sp", x_tm[:, tt_i, :], xs[tt_i * 128:(tt_i + 1) * 128, :], "x%d" % tt_i)
        for ti, ck_ in enumerate((cak, cbk)):
            cv4 = ckl[:, :, ti * 256:(ti + 1) * 256].m(lambda a: a.rearrange("p t (g d) -> p t g d", d=64))
            for kv in range(2):
                for dup in range(2):
                    g_ = 2 * kv + dup
                    dma("sp", cv4.m(lambda a, g_=g_: a[:, :, g_, :]),
                        ck_[:, kv * 64:(kv + 1) * 64].rearrange("(t p) d -> p t d", p=128), "ckl")
        dma("sp", cosT[:, :], cos_d, "c_cos")
        dma("sp", sinT[:, :], sin_d, "c_sin")
        dma("sp", small[0:64, 0:1], qn_d.rearrange("(p o) -> p o", o=1), "c_sm")
        dma("sp", small[64:128, 0:1], qn_d.rearrange("(p o) -> p o", o=1), "c_sm")
        dma("sp", small[0:64, 1:2], kn_d.rearrange("(p o) -> p o", o=1), "c_sm")
        dma("sp", small[64:128, 1:2], kn_d.rearrange("(p o) -> p o", o=1), "c_sm")
        dma("sp", esink[:, :], sink_d.partition_broadcast(128), "c_sink")
        for i, d_ in enumerate((lq1_d, lk1_d, lq2_d, lk2_d)):
            dma("sp", lamt[:, i, :], d_.partition_broadcast(128), "c_lam")
        dma("sp", subln_s[:, :], subln_d.partition_broadcast(128), "c_subln")

        stage("cm1")
        loads = []

        def wsrc(w, c0, n):
            return w[:, c0:c0 + n].rearrange("(k p) n -> p k n", p=128)

        kslot = []
        for base in (512, 1792):
            for kv in range(2):
                for dup in range(2):
                    kslot.append((len(kslot) * 64, wsrc(w_in_e, base + kv * 64, 64), 64))
        loads.append([(0, wsrc(w_in_e, 0, 512), 512)])
        loads.append(kslot)
        loads.append([(0, wsrc(w_in_e, 1280, 512), 512)])
        loads.append([(0, wsrc(w_in_e, 640, 128), 128), (128, wsrc(w_in_e, 1920, 128), 128)])
        loads.append([(0, wsrc(w_in_e, 768, 512), 512)])
        loads.append([(0, wsrc(w_in_e, 2048, 512), 512)])
        loads.append([(0, wsrc(w_out_e, 0, 512), 512)])
        loads.append([(0, wsrc(w_out_e, 512, 512), 512)])
        for i in range(8):
            loads.append([(0, wsrc(w_in_o, 512 * i, 512), 512)])
        loads.append([(0, wsrc(w_out_o, 0, 512), 512)])
        loads.append([(0, wsrc(w_out_o, 512, 512), 512)])
        wstate = {"next": 0}

        def issue_load():
            i = wstate["next"]
            if i >= len(loads):
                return
            wstate["next"] += 1
            sl = i % 3
            for (c0, src, n) in loads[i]:
                dma("pool", wring[sl][:, :, c0:c0 + n], src, "w%d" % sl)

        def W(i):
            return wring[i % 3]

        stage("c0")
        b0 = bank()
        tr(P(b0, 0, 28), stk_all, ident[0:28, 0:28])
        act(scT[:, :].m(lambda a: a.rearrange("p (k v) -> p v k", v=2)),
            P(b0, 0, 16).m(lambda a: a.rearrange("p (v k) -> p v k", v=2)), AF.Silu)
        cp("dve", bmT[:, :], P(b0, 16, 28))
        stage("a1")
        b1 = bank()
        for l in range(2):
            for t in range(6):
                c0 = t * 4 + l * 2
                for k in range(8):
                    mm(P(b1, c0, c0 + 2), wm[l][:, k, t * 128:(t + 1) * 128], scT[:, 2 * k:2 * k + 2],
                       start=(k == 0), stop=(k == 7))
        tt("dve", resm[:, :].m(lambda a: a.rearrange("p (g v) -> p g v", v=2)),
           P(b1, 0, 24).m(lambda a: a.rearrange("p (g v) -> p g v", v=2)),
           bmT[:, :].m(lambda a: a.unsqueeze(2).to_broadcast([128, 12, 2])), ALU.add)
        stage("a2")
        b2 = bank()
        tr(P(b2, 0, 128, 0, 24), resm[:, :], ident[:, :])
        cp("dve", resT[0:24, :], P(b2, 0, 128, 0, 24))
        dma("sp", mg_in.ap(), resT[0:24, :], "mgi", writes=[("dram", "mg_in")])
        def xT_phase(l, raw):
            for j2 in range(0, 8, 2):
                bs = bank()
                bA = [bank(), bank()]
                for jj in range(2):
                    j = j2 + jj
                    for i, t_ in enumerate((2, 3, 4, 5)):
                        tr(P(bA[jj], i * 128, (i + 1) * 128), x_tm[:, t_, j * 128:(j + 1) * 128], ident[:, :])
                    for t_ in range(2):
                        tr(P(bs, jj * 256 + t_ * 128, jj * 256 + (t_ + 1) * 128), x_tm[:, t_, j * 128:(j + 1) * 128],
                           ident[:, :])
                for jj in range(2):
                    j = j2 + jj
                    eA, eB = ("act", "dve") if jj == 0 else ("dve", "act")
                    for (eng_, dst_, src_, v_) in ((eA, actT[:, j, 256:768], P(bA[jj]), 0),
                                                   (eB, actT[:, j, 0:256], P(bs, jj * 256, jj * 256 + 256), 1)):
                        if raw:
                            cp(eng_, dst_, src_)
                        elif eng_ == "act":
                            act(dst_, src_, AF.Identity, scale=scale_ap(j, l, v_), bias=shift_ap(j, l, v_))
                        else:
                            ts("dve", dst_, src_, scale_ap(j, l, v_), shift_ap(j, l, v_), ALU.mult, ALU.add)

        stage("a3")
        allgather_([[0, 1, 2, 3], [4, 5, 6, 7]], mg_in, mg_out, ("dram", "mg_in"), ("dram", "mg_out"))
        stage("a4")
        issue_load()
        dma("pool", Rm[:, :], rmat_d, "c_rm")
        dma("pool", BD[:, :], bd_d, "c_bd")
        issue_load()
        issue_load()
        dma("pool", maskT[:, :], mask_d, "c_mask")
        xT_phase(0, True)
        dma("sp", mo[0:96, :], mg_out.ap(), "mgo", reads=[("dram", "mg_out")])
        b3 = bank()
        tr(P(b3, 0, 96), mo[0:96, :], ident[0:96, 0:96])
        cp("dve", modfm[:, :], P(b3, 0, 96))
        ts("dve", onep[:, :], modfm[:, 32:64], 1.0, None, ALU.add)

        stage("adaln")

        def shift_ap(j, l, v):
            c = j * 4 + l * 2 + v
            return modfm[:, c:c + 1]

        def scale_ap(j, l, v):
            c = j * 4 + l * 2 + v
            return onep[:, c:c + 1]

        act(esink[:, :], esink[:, :], AF.Exp)
        stage("m1")
        for i in range(2):
            tt("dve", lamt[:, 2 * i, :], lamt[:, 2 * i, :], lamt[:, 2 * i + 1, :], ALU.mult)
            treduce_(small[:, 3 + i:4 + i], lamt[:, 2 * i, :])
        stage("m2")
        act(small[:, 3:5], small[:, 3:5], AF.Exp)
        stage("m3")
        tt("dve", small[:, 2:3], small[:, 4:5], small[:, 3:4], ALU.subtract)
        ts("dve", small[:, 2:3], small[:, 2:3], -LAM_INIT, None, ALU.add)
        ts("dve", subln_s[:, :], subln_s[:, :], 1.0 - LAM_INIT, None, ALU.mult)

        stage("misc")
        HALF = [(0, 512), (512, 768)]

        for l in range(2):
            even = (l == 0)
            NG = 4 if even else 8
            DV = 64 if even else 128
            VW = DV + 1
            NKT = 4 if even else 8
            li0 = 0 if even else 8

            def vview(reg, nt):
                return reg.m(lambda a: a[:, :, 0:NG * VW].rearrange("p t (g w) -> p t g w", w=VW))

            dma("sp", lng[:, :], ln_g_d[l].partition_broadcast(128), "c_lng")
            dma("sp", lnb[:, :], ln_b_d[l].partition_broadcast(128), "c_lnb")
            for v in range(2):
                dma("sp", gate_bc[v][:, :].m(lambda a: a.rearrange("p (j q) -> p j q", q=128)),
                    mg_out.ap()[64 + l * 2 + v:96:4, :].partition_broadcast(128), "c_gate%d" % v,
                    reads=[("dram", "mg_out")])
            stage("lc%d" % l)
            memset_("pool", vview(v_sb[:, :, :], 6).m(lambda a: a[:, :, :, DV:VW]), 1.0)
            memset_("pool", vview(vctx[:, :, :], 2).m(lambda a: a[:, :, :, DV:VW]), 1.0)

            stage("ms%d" % l)
            if l == 0:
                for j in range(8):
                    eA, eB = ("act", "dve") if j % 2 == 0 else ("dve", "act")
                    for (eng_, reg_, sa_, sh_) in ((eA, actT[:, j, 256:768], scale_ap(j, l, 0), shift_ap(j, l, 0)),
                                                   (eB, actT[:, j, 0:256], scale_ap(j, l, 1), shift_ap(j, l, 1))):
                        if eng_ == "act":
                            act(reg_, reg_, AF.Identity, scale=sa_, bias=sh_)
                        else:
                            ts("dve", reg_, reg_, sa_, sh_, ALU.mult, ALU.add)
            else:
                xT_phase(l, False)
            for t_ in range(6):
                ts("pool", x_tm[:, t_, :], x_tm[:, t_, :], ALPHA, 0.0, ALU.mult, ALU.add)
            stage("xT%d" % l)
            if even:
                for ti, (ck_, base) in enumerate(((cak, 0), (cbk, 2))):
                    for kv in range(2):
                        bb = bank()
                        for t_ in range(2):
                            tr(P(bb, t_ * 128, (t_ + 1) * 128),
                               ckl[:, t_, ti * 256 + kv * 128:ti * 256 + (kv + 1) * 128], ident[:, :])
                        cp("dve", kctx[:, base + kv, :], P(bb, 0, 256))
                for (cv_, g0) in ((cav, 0), (cbv, 2)):
                    for t_ in range(2):
                        dma("pool", vview(vctx[:, :, :], 2).m(lambda a, g0=g0, t_=t_: a[:, t_, g0:g0 + 2, 0:DV]),
                            cv_[t_ * 128:(t_ + 1) * 128, :].rearrange("p (g d) -> p g d", d=64), "vctx")
                for t_ in range(2):
                    dma("sp", ckl[:, t_, :], cck[t_ * 128:(t_ + 1) * 128, :], "ckl")
            else:
                for hh in range(8):
                    bb = bank()
                    for t_ in range(2):
                        tr(P(bb, t_ * 128, (t_ + 1) * 128), ckl[:, t_, hh * 128:(hh + 1) * 128], ident[:, :])
                    cp("dve" if hh % 2 else "act", kctx[:, hh, :], P(bb, 0, 256))
                for t_ in range(2):
                    dma("pool", vview(vctx[:, :, :], 2).m(lambda a, t_=t_: a[:, t_, :, 0:DV]),
                        ccv[t_ * 128:(t_ + 1) * 128, :].rearrange("p (g d) -> p g d", d=128), "vctx")

            stage("ctx%d" % l)
            cnt = {"fm": 0, "ko": 0, "vo": 0, "rope": 0}

            def rope(dst):
                i = cnt["rope"] % 2
                cnt["rope"] += 1
                bb = bankC()
                mm(P(bb, 0, 256), Rm[:, :], dst)
                tt("dve", rt1[i][:, :], P(bb, 0, 256), sinT[:, :], ALU.mult)
                tt("pool", rt2[i][:, :], dst, cosT[:, :], ALU.mult)
                tt("dve", dst, rt1[i][:, :], rt2[i][:, :], ALU.add)

            def sq_only(bks, i):
                act(sq[i][:, 0:512], P(bks[0]), AF.Square)
                act(sq[i][:, 512:768], P(bks[1], 0, 256), AF.Square)

            def rms_rest(i):
                c0, c1 = bankB(), bankB()
                mm(P(c0), BD[:, :], sq[i][:, 0:512])
                mm(P(c1, 0, 256), BD[:, :], sq[i][:, 512:768])
                act(rs[i][:, 0:512], P(c0), AF.Ln, scale=1.0 / 64, bias=small[:, 5:6])
                act(rs[i][:, 512:768], P(c1, 0, 256), AF.Ln, scale=1.0 / 64, bias=small[:, 5:6])
                act(rs[i][:, :], rs[i][:, :], AF.Exp, scale=-0.5)

            def fm_tile(li, ci, kind, dtile, norm, kout=None):
                w = W(li)
                i = cnt["fm"] % 3
                cnt["fm"] += 1
                bks = [bankA(), bankA()]
                for k in range(8):
                    for hb, (t0, t1) in enumerate(HALF):
                        mm(P(bks[hb], 0, t1 - t0), w[:, k, ci * 128:(ci + 1) * 128], actT[:, k, t0:t1],
                           start=(k == 0), stop=(k == 7))
                if kind == "q":
                    cp("act", qT[:, dtile, 0:512], P(bks[0]))
                    cp("dve", qT[:, dtile, 512:768], P(bks[1], 0, 256))
                    if norm:
                        sq_only(bks, i)

                    def postB_q():
                        if norm:
                            rms_rest(i)
                            stt_(qT[:, dtile, :], qT[:, dtile, :], small[:, 0:1], rs[i][:, :], ALU.mult, ALU.mult)

                    def postC_q():
                        rope(qT[:, dtile, 0:256])
                    return [postB_q, postC_q]
                cp("act", kf[i][:, 0:512], P(bks[0]))
                cp("dve", kf[i][:, 512:768], P(bks[1], 0, 256))
                if norm:
                    sq_only(bks, i)

                def postB_k():
                    if norm:
                        rms_rest(i)
                        stt_(kf[i][:, :], kf[i][:, :], small[:, 1:2], rs[i][:, :], ALU.mult, ALU.mult)

                def post_k():
                    cp("pool", kT[:, dtile, :], kf[i][:, :])
                    rope(kT[:, dtile, 0:256])
                    out_d, c0, wd = kout
                    bb = bankC()
                    for t_ in range(4):
                        tr(P(bb, t_ * 128, (t_ + 1) * 128), kf[i][:, 256 + t_ * 128:256 + (t_ + 1) * 128], ident[:, :])
                    oi = cnt["ko"] % 2
                    cnt["ko"] += 1
                    cp("dve", ko[oi][:, :, 0:wd], P(bb).m(lambda a: a.rearrange("p (t c) -> p t c", c=128)[:, :, 0:wd]))
                    dma("sp", out_d[:, c0:c0 + wd].rearrange("(t p) c -> p t c", p=128), ko[oi][:, :, 0:wd],
                        "ko%d" % oi, is_out=True)
                return [postB_k, post_k]

            def tm_v(li, ncol, gsel, outs):
                w = W(li)
                for t_ in range(6):
                    bb = bank()
                    for k in range(8):
                        mm(P(bb, 0, ncol), actT[:, k, t_ * 128:(t_ + 1) * 128], w[:, k, 0:ncol],
                           start=(k == 0), stop=(k == 7))
                    g0, ng = gsel
                    cp("act", vview(v_sb[:, t_:t_ + 1, :], 1).m(lambda a: a[:, 0, g0:g0 + ng, 0:DV]),
                       P(bb, 0, ncol).m(lambda a: a.rearrange("p (g d) -> p g d", d=DV)))
                    if t_ >= 2:
                        oi = cnt["vo"] % 3
                        cnt["vo"] += 1
                        cp("dve", vo[oi][:, 0:ncol], P(bb, 0, ncol))
                        for (od, c0, s0, n) in outs:
                            dma("sp", od[(t_ - 2) * 128:(t_ - 1) * 128, c0:c0 + n], vo[oi][:, s0:s0 + n],
                                "vo%d" % oi, is_out=True)

            def tm_g(li, gc0):
                w = W(li)
                for t_ in range(6):
                    bb = bank()
                    for k in range(8):
                        mm(P(bb), actT[:, k, t_ * 128:(t_ + 1) * 128], w[:, k, 0:512], start=(k == 0), stop=(k == 7))
                    act(g_sb[:, t_, gc0:gc0 + 512], P(bb), AF.Silu)
                    if not even:
                        tt("pool", g_sb[:, t_, gc0:gc0 + 512].m(lambda a: a.rearrange("p (h d) -> p h d", d=128)),
                           g_sb[:, t_, gc0:gc0 + 512].m(lambda a: a.rearrange("p (h d) -> p h d", d=128)),
                           subln_s[:, :].m(lambda a: a.unsqueeze(1).to_broadcast([128, 4, 128])), ALU.mult)

            def kick_gather():
                gi_, go_ = g_in[l], g_out[l]
                dma("sp", gi_.ap()[0:NKT * 128, :].rearrange("(t p) n -> p t n", p=128), kT[:, 0:NKT, 0:256], "gin",
                    writes=[("dram", "gin%d" % l)])
                fV = NG * DV // 256
                for t_ in range(2):
                    dma("sp", gi_.ap()[NKT * 128 + t_ * 128 * fV:NKT * 128 + (t_ + 1) * 128 * fV, :]
                        .rearrange("(p f) c -> p (f c)", p=128, f=fV)
                        .rearrange("p (g d) -> p g d", d=DV),
                        vview(v_sb[:, 0:2, :], 2).m(lambda a, t_=t_: a[:, t_, :, 0:DV]), "gin", writes=[("dram", "gin%d" % l)])
                allgather_([[0, 1, 2, 3], [4, 5, 6, 7]], gi_, go_, ("dram", "gin%d" % l), ("dram", "gout%d" % l))


            gview = g_out[l].ap().rearrange("(r x) n -> r x n", r=4)

            pend = []

            def run_fm(*a, **kw):
                p = fm_tile(*a, **kw)
                pend.append(p)
                if len(pend) >= 2 and pend[-2][0] is not None:
                    pend[-2][0]()
                    pend[-2][0] = None
                if len(pend) >= 3 and pend[-3][1] is not None:
                    pend[-3][1]()
                    pend[-3][1] = None

            def flush_fm():
                for p in pend:
                    for j_ in range(2):
                        if p[j_] is not None:
                            p[j_]()
                            p[j_] = None
                del pend[:]

            if even:
                for ci in range(4):
                    run_fm(0, ci, "q", ci, True)
                issue_load()
                for ci in range(4):
                    od = ak_o if ci < 2 else bk_o
                    run_fm(1, ci, "k", ci, ci < 2, kout=(od, (ci % 2) * 64, 64))
                issue_load()
                for ci in range(4):
                    run_fm(2, ci, "q", 4 + ci, False)
                issue_load()
                flush_fm()
                tm_v(3, 256, (0, 4), [(av_o, 0, 0, 128), (bv_o, 0, 128, 128)])
                issue_load()
                kick_gather()
                tm_g(4, 0)
                issue_load()
                tm_g(5, 512)
                issue_load()
            else:
                for h_ in range(2):
                    for ci in range(4):
                        run_fm(8 + h_, ci, "q", 4 * h_ + ci, False)
                    issue_load()
                for h_ in range(2):
                    for ci in range(4):
                        run_fm(10 + h_, ci, "k", 4 * h_ + ci, False, kout=(ck_o, (4 * h_ + ci) * 128, 128))
                    issue_load()
                flush_fm()
                for h_ in range(2):
                    tm_v(12 + h_, 512, (4 * h_, 4), [(cv_o, 512 * h_, 0, 512)])
                    issue_load()
                kick_gather()
                for h_ in range(2):
                    tm_g(14 + h_, 512 * h_)
                    issue_load()

            stage("inproj%d" % l)
            stage("gather%d" % l)
            cntA = {"pt": 0, "u": 0}

            def attention(c, maps, units, hook=None):
                sample = (c == 0)
                nkt = 10 if sample else 2
                oab = oa[c % 2]
                ustate = {}

                def load_unit(u):
                    sl = cntA["u"] % 3
                    cntA["u"] += 1
                    ktile, vg = units[u]
                    dma("sp", kl[sl][:, :].m(lambda a: a.rearrange("p (r n) -> p r n", r=4)),
                        gview[:, ktile * 128:(ktile + 1) * 128, :].rearrange("r p n -> p r n"), "kl%d" % sl,
                        reads=[("dram", "gout%d" % l)])
                    f = NG * DV // 256
                    vsrc = gview[:, NKT * 128:NKT * 128 + 256 * f, :] \
                        .rearrange("r (t p f) c -> p r t (f c)", p=128, f=f)[:, :, :, vg * DV:(vg + 1) * DV]
                    for r in range(4):
                        dma("sp", vl[sl][:, 2 * r:2 * r + 2, 0:DV], vsrc[:, r, :, :], "vl%d" % sl,
                            reads=[("dram", "gout%d" % l)])
                    memset_("pool", vl[sl][:, :, DV:VW], 1.0)
                    ustate[u] = sl

                def kk(mp, kt):
                    b_ = mp["base"]
                    if not sample:
                        return kT[b_:b_ + 64, mp["ktile"], c * 256 + kt * 128:c * 256 + (kt + 1) * 128]
                    if kt < 8:
                        return kl[ustate[mp["unit"]]][b_:b_ + 64, kt * 128:(kt + 1) * 128]
                    return kctx[b_:b_ + 64, mp["ktile"], (kt - 8) * 128:(kt - 7) * 128]

                def vv(mp, kt):
                    g = mp["vg"]
                    if not sample:
                        return v_sb[:, 2 * c + kt, g * VW:(g + 1) * VW]
                    if kt < 8:
                        return vl[ustate[mp["unit"]]][:, kt, 0:VW]
                    return vctx[:, kt - 8, g * VW:(g + 1) * VW]

                def qk(mp):
                    sl = cntA["pt"] % 3
                    cntA["pt"] += 1
                    mp["pt"] = sl
                    b_ = mp["base"]
                    qreg = qT[b_:b_ + 64, mp["qtile"], c * 256:(c + 1) * 256]
                    for kt0 in range(0, nkt, 2):
                        bb = bankS()
                        for d_ in range(2):
                            mm(P(bb, d_ * 256, (d_ + 1) * 256), kk(mp, kt0 + d_), qreg)
                        act(pT[sl][:, kt0:kt0 + 2, :].m(lambda a: a.rearrange("p t q -> p (t q)")), P(bb), AF.Exp,
                            scale=0.125)
                    if sample and mp["mask"]:
                        tt("dve", pT[sl][:, 0:8, :].m(lambda a: a.rearrange("p t q -> p (t q)")),
                           pT[sl][:, 0:8, :].m(lambda a: a.rearrange("p t q -> p (t q)")), maskT[:, :], ALU.mult)

                def pv(mp, bb, col0):
                    sl = mp["pt"]
                    for qt in range(2):
                        for kt in range(nkt):
                            mm(P(bb, col0 + qt * VW, col0 + (qt + 1) * VW), pT[sl][:, kt, qt * 128:(qt + 1) * 128],
                               vv(mp, kt), start=(kt == 0), stop=(kt == nkt - 1))

                def post_ab(grp, bb):
                    h0 = grp[0]["sinkh"]
                    pv4 = P(bb, 0, 4 * VW).m(lambda a: a.rearrange("p (k w) -> p k w", w=VW))
                    rsum = pv4.m(lambda a: a[:, :, DV:VW].rearrange("p k o -> p (k o)"))
                    if grp[0]["sink"]:
                        tt("dve", rr[:, 0:4].m(lambda a: a.rearrange("p (g q) -> p g q", q=2)),
                           rsum.m(lambda a: a.rearrange("p (g q) -> p g q", q=2)),
                           esink[:, h0:h0 + 2].m(lambda a: a.unsqueeze(2).to_broadcast([128, 2, 2])), ALU.add)
                        recip(rr[:, 0:4], rr[:, 0:4])
                    else:
                        recip(rr[:, 0:4], rsum)
                    oc = grp[0]["ocol"]
                    for gi in range(2):
                        tt("dve", oab[:, :, oc + gi * 64:oc + (gi + 1) * 64],
                           P(bb, gi * 2 * VW, (gi + 1) * 2 * VW).m(lambda a: a.rearrange("p (q w) -> p q w", w=VW)[:, :, 0:DV]),
                           rr[:, 2 * gi:2 * gi + 2].m(lambda a: a.unsqueeze(2).to_broadcast([128, 2, DV])), ALU.mult)

                def post_c(grp, bX, bY):
                    hh = grp[0]["sinkh"]
                    X = P(bX, 0, 2 * VW).m(lambda a: a.rearrange("p (q w) -> p q w", w=VW))
                    Y = P(bY, 0, 2 * VW).m(lambda a: a.rearrange("p (q w) -> p q w", w=VW))
                    recip(rr[:, 0:2], X.m(lambda a: a[:, :, DV:VW].rearrange("p q o -> p (q o)")))
                    recip(rr[:, 2:4], Y.m(lambda a: a[:, :, DV:VW].rearrange("p q o -> p (q o)")))
                    ts("dve", rr[:, 4:6], rr[:, 2:4], small[:, 2:3], None, ALU.mult)
                    tt("dve", o1[:, :, :], X.m(lambda a: a[:, :, 0:DV]),
                       rr[:, 0:2].m(lambda a: a.unsqueeze(2).to_broadcast([128, 2, DV])), ALU.mult)
                    for qt in range(2):
                        stt_(oab[:, qt, hh * 128:(hh + 1) * 128], Y.m(lambda a, qt=qt: a[:, qt, 0:DV]),
                             rr[:, 4 + qt:5 + qt], o1[:, qt, :], ALU.mult, ALU.add)

                pend = None
                groups = [maps[i:i + 2] for i in range(0, len(maps), 2)]
                gstate = []
                loaded = set()
                nu = len(units)
                if sample:
                    load_unit(0)
                    loaded.add(0)
                flat = []
                for g in groups:
                    for mp in g:
                        flat.append((mp, g))
                pvbank = {}

                def do_pv(mp, g):
                    gid = id(g)
                    if even:
                        if gid not in pvbank:
                            pvbank[gid] = bankPV()
                        gi = 0 if mp is g[0] else 1
                        pv(mp, pvbank[gid], gi * 2 * VW)
                        if mp is g[-1]:
                            post_ab(g, pvbank[gid])
                    else:
                        if gid not in pvbank:
                            pvbank[gid] = [bankPV(), bankPV()]
                        gi = 0 if mp is g[0] else 1
                        pv(mp, pvbank[gid][gi], 0)
                        if mp is g[-1]:
                            post_c(g, pvbank[gid][0], pvbank[gid][1])

                for idx, (mp, g) in enumerate(flat):
                    if idx == 2 and hook is not None:
                        hook()
                    if sample:
                        u = mp["unit"]
                        if u + 1 < nu and (u + 1) not in loaded:
                            load_unit(u + 1)
                            loaded.add(u + 1)
                    qk(mp)
                    if pend is not None:
                        do_pv(*pend)
                    pend = (mp, g)
                do_pv(*pend)

                if not even:
                    slq = cntA["pt"] % 3
                    cntA["pt"] += 1
                    sqb = pT[slq][:, 0:8, :]
                    act(sqb.m(lambda a: a.rearrange("p t q -> p (t q)")),
                        oab[:, :, :].m(lambda a: a.rearrange("p q d -> p (q d)")), AF.Square)
                    treduce_(ssb[:, 0:16], sqb.m(lambda a: a.rearrange("p t q -> p (t q)").rearrange("p (g d) -> p g d", d=128)))
                    act(ssb[:, 0:16], ssb[:, 0:16], AF.Ln, scale=1.0 / 128, bias=small[:, 5:6])
                    act(ssb[:, 0:16], ssb[:, 0:16], AF.Exp, scale=-0.5)
                    tt("dve", oab[:, :, :].m(lambda a: a.rearrange("p q (h d) -> p (q h) d", d=128)),
                       oab[:, :, :].m(lambda a: a.rearrange("p q (h d) -> p (q h) d", d=128)),
                       ssb[:, 0:16].m(lambda a: a.unsqueeze(2).to_broadcast([128, 16, 128])), ALU.mult)
                tt("pool", oab[:, 0, :], oab[:, 0, :], g_sb[:, 2 * c, :], ALU.mult)
                tt("dve", oab[:, 1, :], oab[:, 1, :], g_sb[:, 2 * c + 1, :], ALU.mult)

                def epi_pe(bank_fn=bank):
                    for qt in range(2):
                        t_ = 2 * c + qt
                        for hf in range(2):
                            bb = bank_fn()
                            for j in range(4):
                                tr(P(bb, j * 128, (j + 1) * 128), oab[:, qt, (4 * hf + j) * 128:(4 * hf + j + 1) * 128],
                                   ident[:, :])
                            cp("act" if hf == 0 else "dve", actT[:, 4 * hf:4 * hf + 4, t_ * 128:(t_ + 1) * 128],
                               P(bb).m(lambda a: a.rearrange("p (j q) -> p j q", q=128)))
                return epi_pe

            def make_maps():
                maps, units = [], []
                if even:
                    for typ in range(2):
                        for kv in range(2):
                            units.append((typ * 2 + kv, typ * 2 + kv))
                            for hq in range(4):
                                h = kv * 4 + hq
                                maps.append(dict(ktile=typ * 2 + kv, qtile=typ * 4 + h // 2, base=(h % 2) * 64,
                                                 vg=typ * 2 + kv, unit=typ * 2 + kv, sinkh=h, sink=(typ == 1),
                                                 mask=(typ == 1), ocol=typ * 512 + (h - h % 2) * 64))
                else:
                    for hh in range(8):
                        units.append((hh, hh))
                        for m_ in range(2):
                            maps.append(dict(ktile=hh, qtile=hh, base=m_ * 64, vg=hh, unit=hh, sinkh=hh, sink=False,
                                             mask=False, ocol=hh * 128))
                return maps, units

            wl = [6, 7] if even else [16, 17]
            yic = {"n": 0}

            def outproj_ln(tiles, bank_fn=bank, split=False):
              for t_ in tiles:
                  v = 1 if t_ < 2 else 0
                  yi = yic["n"]
                  y_ = yt[yi % 2]
                  stt = stts[yi % 2]
                  yic["n"] += 1
                  bks = [bank_fn(), bank_fn()]
                  for hf in range(2):
                      for k in range(8):
                          mm(P(bks[hf]), actT[:, k, t_ * 128:(t_ + 1) * 128], W(wl[hf])[:, k, 0:512],
                             start=(k == 0), stop=(k == 7))
                  for hf in range(2):
                      tt("dve", y_[:, hf * 512:(hf + 1) * 512], P(bks[hf]), gate_bc[v][:, hf * 512:(hf + 1) * 512], ALU.mult)
                  if split:
                      tt("pool", y_[:, 0:512], y_[:, 0:512], x_tm[:, t_, 0:512], ALU.add)
                      tt("dve", y_[:, 512:1024], y_[:, 512:1024], x_tm[:, t_, 512:1024], ALU.add)
                  else:
                      tt("pool", y_[:, :], y_[:, :], x_tm[:, t_, :], ALU.add)
                  for hf in range(2):
                      bnstats_(stt[:, 32 + hf * 6:38 + hf * 6], y_[:, hf * 512:(hf + 1) * 512])
                  bnaggr_(stt[:, 0:2], stt[:, 32:44])
                  act(stt[:, 2:3], stt[:, 1:2], AF.Ln, bias=small[:, 5:6])
                  act(stt[:, 2:3], stt[:, 2:3], AF.Exp, scale=-0.5)
                  ts("dve", stt[:, 3:4], stt[:, 0:1], -1.0, stt[:, 2:3], ALU.mult, ALU.mult)
                  act(y_[:, :], y_[:, :], AF.Identity, scale=stt[:, 2:3], bias=stt[:, 3:4])
                  tt("dve", y_[:, :], y_[:, :], lng[:, :], ALU.mult)
                  if split:
                      tt("pool", x_tm[:, t_, 0:512], y_[:, 0:512], lnb[:, 0:512], ALU.add)
                      tt("dve", x_tm[:, t_, 512:1024], y_[:, 512:1024], lnb[:, 512:1024], ALU.add)
                  else:
                      tt("pool", x_tm[:, t_, :], y_[:, :], lnb[:, :], ALU.add)
                  if l == 1:
                      dma("pool", y_o[t_ * 128:(t_ + 1) * 128, :], x_tm[:, t_, :], "yo%d" % (t_ % 2), is_out=True)
                  elif DBG:
                      dma("sp", dbg_o[t_ * 128:(t_ + 1) * 128, :], x_tm[:, t_, :], "yo%d" % (t_ % 2), is_out=True)

            maps, units = make_maps()
            epi1 = attention(1, maps, units)
            maps, units = make_maps()
            epi2 = attention(2, maps, units)
            epi1()
            outproj_ln((2, 3), split=True)

            def mid_hook():
                epi2(bankS)
                outproj_ln((4, 5), bankS, split=True)
            maps, units = make_maps()
            epi0 = attention(0, maps, units, hook=mid_hook)
            stage("attn%d_0" % l)
            epi0()
            outproj_ln((0, 1), split=True)
            if even:
                for _ in range(2):
                    issue_load()
            else:
                pass


    try:
        body()
    except _Stop:
        pass

    fin = S.op("sp", None, reads=[], writes=[])
    for o in S.out_ops:
        fin.dwaits[o.key] = 16 * S.dcnt[o.key]
    S.finalize()

    sem_names = {}
    import contextlib
    with contextlib.ExitStack() as es:
        csem = {e: es.enter_context(nc.semaphore("c_" + e)) for e in ("pe", "act", "dve", "pool")}
        dsem = {k: es.enter_context(nc.semaphore("d_" + k)) for k in S.dcnt}
        ccsem = [es.enter_context(nc.semaphore("cc%d" % i)) for i in range(S.ncc)]
        block = es.enter_context(nc.Block())

        def run(engname, e):
            known = {}

            def wait(sem, val, tag):
                if known.get(tag, 0) < val:
                    e.wait_ge(sem, val)
                    known[tag] = val

            for o in S.q[engname]:
                need = {}
                for d_ in o.deps:
                    if d_.kind == "c":
                        tag = ("c", d_.eng)
                        need[tag] = max(need.get(tag, 0), d_.sig)
                    elif d_.kind == "cc":
                        need[("cc", d_.ccid)] = 1
                for k_, v_ in o.dwaits.items():
                    need[("d", k_)] = max(need.get(("d", k_), 0), v_)
                for tag, val in need.items():
                    if tag[0] == "c":
                        wait(csem[tag[1]], val, tag)
                    elif tag[0] == "d":
                        wait(dsem[tag[1]], val, tag)
                    else:
                        wait(ccsem[tag[1]], val, tag)
                if o.fn is None:
                    continue
                ins = o.fn(e)
                if o.kind == "c":
                    if o.need_inc:
                        ins.then_inc(csem[engname], 1)
                elif o.kind == "d":
                    ins.then_inc(dsem[o.key], 16)
                else:
                    ins.then_inc(ccsem[o.ccid])

        @block.tensor
        def _(e):
            run("pe", e)

        @block.scalar
        def _(e):
            run("act", e)

        @block.vector
        def _(e):
            run("dve", e)

        @block.gpsimd
        def _(e):
            run("pool", e)

        @block.sync
        def _(e):
            run("sp", e)
    return nc


def _consts(r):
    ident = np.eye(128, dtype=np.float32)
    rmat = np.zeros((128, 128), np.float32)
    for m in range(128):
        if (m % 32) < 16:
            rmat[m + 16, m] = -1.0
        else:
            rmat[m - 16, m] = 1.0
    bd = np.zeros((128, 128), np.float32)
    bd[0:64, 0:64] = 1.0
    bd[64:128, 64:128] = 1.0
    t = 256 * r + np.arange(256)
    row = (t // 64).astype(np.float64)
    col = (t % 64).astype(np.float64)
    freqs = (10000.0 ** (-(np.arange(16, dtype=np.float32) / np.float32(16)))).astype(np.float64)
    ang = np.zeros((128, 256))
    for p in range(128):
        d = p % 64
        f = freqs[d % 16]
        ang[p] = (row if d < 32 else col) * f
    cosT = np.cos(ang).astype(np.float32)
    sinT = np.sin(ang).astype(np.float32)
    j = (np.arange(8)[None, :, None] * 128 + np.arange(128)[:, None, None])
    i = (256 * r + np.arange(256))[None, None, :]
    mask = (np.abs(j - i) <= 128).astype(np.float32).reshape(128, 2048)
    return dict(ident=ident, rmat=rmat, bdmat=bd, cosT=cosT, sinT=sinT, maskd=mask)


_NC_CACHE = {}


def make_in_maps(x_prompt, x_sample, cache_a_k, cache_a_v, cache_b_k, cache_b_v, cache_c_k, cache_c_v,
                 c, c_ctx, w_mod, b_mod, ln_g, ln_b, w_in_even, w_out_even, q_norm_a, k_norm_a, sink_b,
                 w_in_odd, w_out_odd, lambda_q1, lambda_k1, lambda_q2, lambda_k2, subln_c):
    f = lambda a: np.ascontiguousarray(np.asarray(a, dtype=np.float32))
    x_prompt, x_sample = f(x_prompt), f(x_sample)
    in_maps = []
    for i in range(8):
        b, r = i // 4, i % 4
        m = dict(
            xs=np.concatenate([x_sample[b, 256 * r:256 * (r + 1)], x_prompt[2 * i], x_prompt[2 * i + 1]], 0),
            cak=f(cache_a_k)[b, 0].reshape(256, 128), cav=f(cache_a_v)[b, 0].reshape(256, 128),
            cbk=f(cache_b_k)[b, 0].reshape(256, 128), cbv=f(cache_b_v)[b, 0].reshape(256, 128),
            cck=f(cache_c_k)[b, 0].reshape(256, 1024), ccv=f(cache_c_v)[b, 0].reshape(256, 1024),
            cvec=np.stack([f(c_ctx), f(c)[b]], 0),
            wmod_sh=f(f(w_mod)[:, :, 768 * r:768 * (r + 1)]), bmod_sh=f(f(b_mod)[:, 768 * r:768 * (r + 1)]),
            ln_g=f(ln_g), ln_b=f(ln_b),
            w_in_e=f(w_in_even)[0], w_out_e=f(w_out_even)[0], w_in_o=f(w_in_odd)[0], w_out_o=f(w_out_odd)[0],
            qn=f(q_norm_a)[0], kn=f(k_norm_a)[0], sink=f(sink_b)[0],
            lq1=f(lambda_q1)[0], lk1=f(lambda_k1)[0], lq2=f(lambda_q2)[0], lk2=f(lambda_k2)[0],
            subln=f(subln_c)[0],
        )
        m.update(_consts(r))
        in_maps.append({k: np.ascontiguousarray(v) for k, v in m.items()})
    return in_maps


def assemble(R):
    y_prompt = np.zeros((16, 256, 1024), np.float32)
    y_sample = np.zeros((2, 1024, 1024), np.float32)
    nak = np.zeros((16, 1, 256, 2, 64), np.float32)
    nav, nbk, nbv = np.zeros_like(nak), np.zeros_like(nak), np.zeros_like(nak)
    nck = np.zeros((16, 1, 256, 8, 128), np.float32)
    ncv = np.zeros_like(nck)
    for i in range(8):
        b, r = i // 4, i % 4
        y = R[i]["y"]
        y_sample[b, 256 * r:256 * (r + 1)] = y[0:256]
        for s in range(2):
            y_prompt[2 * i + s] = y[256 * (s + 1):256 * (s + 2)]
            sl = slice(256 * s, 256 * (s + 1))
            nak[2 * i + s, 0] = R[i]["ak"][sl].reshape(256, 2, 64)
            nav[2 * i + s, 0] = R[i]["av"][sl].reshape(256, 2, 64)
            nbk[2 * i + s, 0] = R[i]["bk"][sl].reshape(256, 2, 64)
            nbv[2 * i + s, 0] = R[i]["bv"][sl].reshape(256, 2, 64)
            nck[2 * i + s, 0] = R[i]["ck"][sl].reshape(256, 8, 128)
            ncv[2 * i + s, 0] = R[i]["cv"][sl].reshape(256, 8, 128)
    return (y_prompt, y_sample, nak, nav, nbk, nbv, nck, ncv)


def kernel(**inputs):
    in_maps = make_in_maps(**inputs)
    if "nc" not in _NC_CACHE:
        _NC_CACHE["nc"] = build_nc()
    res = run_bass_kernel_spmd(_NC_CACHE["nc"], in_maps, core_ids=list(range(8)))
    return assemble(res.results)
```
